# Optimizing a Trainium2 kernel written in Bass

```python
import jax, jax.numpy as jnp
from jax import lax
import numpy as np

D_MODEL = 1024
BATCH = 8
SEQ = 8192
DEPTH = 2
DEC_BATCH = 16
DEC_SEQ = 64
PAST_LEN = 4096

CHUNK = 64
N_MIXERS = 2
N_RET = (DEPTH + 1) // 2
N_SB = DEPTH // 2
RET_HEADS = 4
RET_DK = D_MODEL // RET_HEADS
RET_DV = 2 * RET_DK
RET_QK = RET_HEADS * RET_DK
RET_VW = RET_HEADS * RET_DV
ROPE_BASE = 10000.0
SB_HEADS = 16
SB_HD = D_MODEL // SB_HEADS
SB_QBLOCK = 128
PEER_HEADS = 8
PEER_QDIM = 256
N_KEYS = 128
N_EXPERTS = N_KEYS * N_KEYS
PEER_TOPK = 16
PEER_TOKEN_BLOCK = 256
LN_EPS = 1e-5
GN_EPS = 1e-6
ALPHA = (2 * DEPTH) ** 0.25
BETA_INIT = (8 * DEPTH) ** -0.25

kernel_name = 'retnet_stickbreak_peer_stream_step'


def layer_norm(x, g, b):
    xf = x.astype(jnp.float32)
    mu = jnp.mean(xf, axis=-1, keepdims=True)
    var = jnp.mean(jnp.square(xf - mu), axis=-1, keepdims=True)
    y = (xf - mu) * lax.rsqrt(var + LN_EPS) * g.astype(jnp.float32) + b.astype(jnp.float32)
    return y.astype(x.dtype)


def rotary(x, pos):
    half = x.shape[-1] // 2
    inv = ROPE_BASE ** (-jnp.arange(half, dtype=jnp.float32) / half)
    ang = pos.astype(jnp.float32)[:, None] * inv[None, :]
    cos, sin = jnp.cos(ang), jnp.sin(ang)
    xf = x.astype(jnp.float32)
    x1, x2 = xf[..., :half], xf[..., half:]
    return jnp.concatenate([x1 * cos - x2 * sin, x1 * sin + x2 * cos], axis=-1).astype(x.dtype)


def retention_log_decay():
    return jnp.log1p(-jnp.exp2(-5.0 - jnp.arange(RET_HEADS, dtype=jnp.float32)))


def retention_chunk(S, q, k, v, log_g):
    L = q.shape[2]
    pos = jnp.arange(L, dtype=jnp.float32)
    diff = pos[:, None] - pos[None, :]
    causal = diff >= 0
    dmask = jnp.where(causal, jnp.exp(log_g[:, None, None] * jnp.where(causal, diff, 0.0)), 0.0).astype(q.dtype)
    inner = jnp.einsum('bhnm,bhme->bhne', jnp.einsum('bhnd,bhmd->bhnm', q, k) * dmask, v)
    q_dec = jnp.exp(log_g[:, None] * (pos + 1.0)[None, :]).astype(q.dtype)
    cross = jnp.einsum('bhnd,bhde->bhne', q * q_dec[None, :, :, None], S)
    k_dec = jnp.exp(log_g[:, None] * (L - 1.0 - pos)[None, :]).astype(q.dtype)
    s_dec = jnp.exp(log_g * L).astype(S.dtype)
    S_new = S * s_dec[None, :, None, None] + jnp.einsum('bhmd,bhme->bhde', k * k_dec[None, :, :, None], v)
    return S_new.astype(S.dtype), inner + cross


def retention_mixer(x, S0, pos0, w_in, w_o):
    B, L, _ = x.shape
    q, k, v, g = jnp.split(x @ w_in, [RET_QK, 2 * RET_QK, 2 * RET_QK + RET_VW], axis=-1)
    pos = pos0 + jnp.arange(L)
    q = rotary(q.reshape(B, L, RET_HEADS, RET_DK).transpose(0, 2, 1, 3), pos)
    k = rotary(k.reshape(B, L, RET_HEADS, RET_DK).transpose(0, 2, 1, 3), pos) * (RET_DK ** -0.5)
    v = v.reshape(B, L, RET_HEADS, RET_DV).transpose(0, 2, 1, 3)
    log_g = retention_log_decay()
    if L <= CHUNK:
        S, o = retention_chunk(S0, q, k, v, log_g)
    else:
        nc = L // CHUNK
        def to_chunks(t):
            return jnp.moveaxis(t.reshape(B, RET_HEADS, nc, CHUNK, t.shape[-1]), 2, 0)
        S, o = lax.scan(lambda s, c: retention_chunk(s, c[0], c[1], c[2], log_g), S0,
                        (to_chunks(q), to_chunks(k), to_chunks(v)))
        o = jnp.moveaxis(o, 0, 2).reshape(B, RET_HEADS, L, RET_DV)
    of = o.astype(jnp.float32)
    mu = jnp.mean(of, axis=-1, keepdims=True)
    var = jnp.mean(jnp.square(of - mu), axis=-1, keepdims=True)
    y = ((of - mu) * lax.rsqrt(var + GN_EPS)).transpose(0, 2, 1, 3).reshape(B, L, RET_VW).astype(x.dtype)
    return (jax.nn.silu(g) * y) @ w_o, S


def stick_breaking_block(q, k, v, q_pos, k_pos):
    z = jnp.einsum('bhqd,bhkd->bhqk', q, k).astype(jnp.float32) * (SB_HD ** -0.5)
    mask = k_pos[None, :] < q_pos[:, None]
    log_remain = jnp.where(mask, jax.nn.log_sigmoid(-z), 0.0)
    log_pass = lax.cumsum(log_remain, axis=3, reverse=True) - log_remain
    w = jnp.where(mask, jnp.exp(jax.nn.log_sigmoid(z) + log_pass), 0.0)
    return jnp.einsum('bhqk,bhkd->bhqd', w.astype(v.dtype), v)


def stick_breaking_mixer(x, k_past, v_past, pos0, w_qkv, w_o):
    B, L, _ = x.shape
    q, k, v = jnp.split(x @ w_qkv, 3, axis=-1)
    q = q.reshape(B, L, SB_HEADS, SB_HD).transpose(0, 2, 1, 3)
    k = k.reshape(B, L, SB_HEADS, SB_HD).transpose(0, 2, 1, 3)
    v = v.reshape(B, L, SB_HEADS, SB_HD).transpose(0, 2, 1, 3)
    if k_past is None:
        k_all, v_all = k, v
    else:
        k_all = jnp.concatenate([k_past, k], axis=2)
        v_all = jnp.concatenate([v_past, v], axis=2)
    q_pos = pos0 + jnp.arange(L)
    k_pos = jnp.arange(k_all.shape[2])
    if L <= SB_QBLOCK:
        o = stick_breaking_block(q, k_all, v_all, q_pos, k_pos)
    else:
        nb = L // SB_QBLOCK
        qb = jnp.moveaxis(q.reshape(B, SB_HEADS, nb, SB_QBLOCK, SB_HD), 2, 0)
        pb = q_pos.reshape(nb, SB_QBLOCK)
        o = lax.map(lambda a: stick_breaking_block(a[0], k_all, v_all, a[1], k_pos), (qb, pb))
        o = jnp.moveaxis(o, 0, 2).reshape(B, SB_HEADS, L, SB_HD)
    out = o.transpose(0, 2, 1, 3).reshape(B, L, D_MODEL) @ w_o
    return out, k, v


def peer_tokens(xt, w_q, keys_a, keys_b, u, v):
    T = xt.shape[0]
    q = (xt @ w_q).reshape(T, PEER_HEADS, PEER_QDIM)
    qa, qb = jnp.split(q, 2, axis=-1)
    sa = jnp.einsum('thd,hkd->thk', qa, keys_a)
    sb = jnp.einsum('thd,hkd->thk', qb, keys_b)
    va, ia = lax.top_k(sa, PEER_TOPK)
    vb, ib = lax.top_k(sb, PEER_TOPK)
    cand_s = (va[..., :, None] + vb[..., None, :]).reshape(T, PEER_HEADS, PEER_TOPK * PEER_TOPK)
    cand_i = (ia[..., :, None] * N_KEYS + ib[..., None, :]).reshape(T, PEER_HEADS, PEER_TOPK * PEER_TOPK)
    s, j = lax.top_k(cand_s, PEER_TOPK)
    idx = jnp.take_along_axis(cand_i, j, axis=-1)
    gate = jax.nn.softmax(s.astype(jnp.float32), axis=-1).astype(xt.dtype)
    h = jax.nn.gelu(jnp.einsum('td,thkd->thk', xt, u[idx]), approximate=False)
    return jnp.einsum('thk,thkd->td', gate * h, v[idx])


def peer_ffn(x, w_q, keys_a, keys_b, u, v):
    B, L, D = x.shape
    T = B * L
    xt = x.reshape(T, D)
    if T <= PEER_TOKEN_BLOCK:
        out = peer_tokens(xt, w_q, keys_a, keys_b, u, v)
    else:
        nb = -(-T // PEER_TOKEN_BLOCK)
        pad = nb * PEER_TOKEN_BLOCK - T
        xb = jnp.pad(xt, ((0, pad), (0, 0))).reshape(nb, PEER_TOKEN_BLOCK, D)
        out = lax.map(lambda a: peer_tokens(a, w_q, keys_a, keys_b, u, v), xb)
        out = out.reshape(nb * PEER_TOKEN_BLOCK, D)[:T]
    return out.reshape(B, L, D)


def setup_inputs(seed: int = 0) -> dict:
    key = jax.random.key(seed)
    ks = jax.random.split(key, 16)
    f32 = jnp.float32

    def nrm(k, shape, scale):
        return jax.random.normal(k, shape, f32) * scale

    d_in = D_MODEL ** -0.5
    x_prompt = nrm(ks[0], (BATCH, SEQ, D_MODEL), 1.0)
    x_sample = nrm(ks[1], (DEC_BATCH, DEC_SEQ, D_MODEL), 1.0)
    state_ret = nrm(ks[2], (N_RET, DEC_BATCH, RET_HEADS, RET_DK, RET_DV), 0.5)
    cache_k = nrm(ks[3], (N_SB, DEC_BATCH, SB_HEADS, PAST_LEN, SB_HD), 1.0)
    cache_v = nrm(ks[4], (N_SB, DEC_BATCH, SB_HEADS, PAST_LEN, SB_HD), 1.0)
    ret_cols = jnp.concatenate([jnp.full((2 * RET_QK,), d_in, f32),
                                jnp.full((RET_VW,), d_in * BETA_INIT, f32),
                                jnp.full((RET_VW,), d_in, f32)])
    ret_w_in = nrm(ks[5], (N_RET, D_MODEL, 2 * RET_QK + 2 * RET_VW), 1.0) * ret_cols
    ret_w_o = nrm(ks[6], (N_RET, RET_VW, D_MODEL), RET_VW ** -0.5 * BETA_INIT)
    sb_cols = jnp.concatenate([jnp.full((2 * D_MODEL,), d_in, f32),
                               jnp.full((D_MODEL,), d_in * BETA_INIT, f32)])
    sb_w_qkv = nrm(ks[7], (N_SB, D_MODEL, 3 * D_MODEL), 1.0) * sb_cols
    sb_w_o = nrm(ks[8], (N_SB, D_MODEL, D_MODEL), d_in * BETA_INIT)
    peer_w_q = nrm(ks[9], (DEPTH, D_MODEL, PEER_HEADS * PEER_QDIM), d_in)
    peer_keys_a = nrm(ks[10], (DEPTH, PEER_HEADS, N_KEYS, PEER_QDIM // 2), (PEER_QDIM // 2) ** -0.5)
    peer_keys_b = nrm(ks[11], (DEPTH, PEER_HEADS, N_KEYS, PEER_QDIM // 2), (PEER_QDIM // 2) ** -0.5)
    peer_u = nrm(ks[12], (DEPTH, N_EXPERTS, D_MODEL), d_in)
    peer_v = nrm(ks[13], (DEPTH, N_EXPERTS, D_MODEL), BETA_INIT * PEER_HEADS ** -0.5)
    ln_g = 1.0 + nrm(ks[14], (DEPTH, 2, D_MODEL), 0.02)
    ln_b = nrm(ks[15], (DEPTH, 2, D_MODEL), 0.02)
    return {'x_prompt': x_prompt, 'x_sample': x_sample, 'state_ret': state_ret,
            'cache_k': cache_k, 'cache_v': cache_v,
            'ret_w_in': ret_w_in, 'ret_w_o': ret_w_o, 'sb_w_qkv': sb_w_qkv, 'sb_w_o': sb_w_o,
            'peer_w_q': peer_w_q, 'peer_keys_a': peer_keys_a, 'peer_keys_b': peer_keys_b,
            'peer_u': peer_u, 'peer_v': peer_v, 'ln_g': ln_g, 'ln_b': ln_b}


def reference(x_prompt, x_sample, state_ret, cache_k, cache_v, ret_w_in, ret_w_o, sb_w_qkv, sb_w_o,
              peer_w_q, peer_keys_a, peer_keys_b, peer_u, peer_v, ln_g, ln_b):
    xp, xs = x_prompt, x_sample
    ret_p, ret_s, kp, vp, ksm, vsm = [], [], [], [], [], []
    for i in range(DEPTH):
        j = i // N_MIXERS
        if i % N_MIXERS == 0:
            s0 = jnp.zeros((xp.shape[0], RET_HEADS, RET_DK, RET_DV), xp.dtype)
            mp, sp = retention_mixer(xp, s0, 0, ret_w_in[j], ret_w_o[j])
            ms, ss = retention_mixer(xs, state_ret[j], PAST_LEN, ret_w_in[j], ret_w_o[j])
            ret_p.append(sp)
            ret_s.append(ss)
        else:
            mp, k_new_p, v_new_p = stick_breaking_mixer(xp, None, None, 0, sb_w_qkv[j], sb_w_o[j])
            ms, k_new_s, v_new_s = stick_breaking_mixer(xs, cache_k[j], cache_v[j], PAST_LEN, sb_w_qkv[j], sb_w_o[j])
            kp.append(k_new_p)
            vp.append(v_new_p)
            ksm.append(k_new_s)
            vsm.append(v_new_s)
        xp = layer_norm(ALPHA * xp + mp, ln_g[i, 0], ln_b[i, 0])
        xs = layer_norm(ALPHA * xs + ms, ln_g[i, 0], ln_b[i, 0])
        xp = layer_norm(ALPHA * xp + peer_ffn(xp, peer_w_q[i], peer_keys_a[i], peer_keys_b[i], peer_u[i], peer_v[i]),
                        ln_g[i, 1], ln_b[i, 1])
        xs = layer_norm(ALPHA * xs + peer_ffn(xs, peer_w_q[i], peer_keys_a[i], peer_keys_b[i], peer_u[i], peer_v[i]),
                        ln_g[i, 1], ln_b[i, 1])
    return (xp, xs, jnp.stack(ret_p), jnp.stack(ret_s), jnp.stack(kp), jnp.stack(vp), jnp.stack(ksm), jnp.stack(vsm))
```

```python
import math
from contextlib import ExitStack

import numpy as np
import concourse.bass as bass
import concourse.mybir as mybir
from concourse.bass_utils import run_bass_kernel_spmd

F32 = mybir.dt.float32
BF16 = mybir.dt.bfloat16
U32 = mybir.dt.uint32
I32 = mybir.dt.int32
AF = mybir.ActivationFunctionType
ALU = mybir.AluOpType
AX = mybir.AxisListType

D = 1024
NCORES = 8
DEC_SEQ = 64
ALPHA = 4 ** 0.25
LN_EPS = 1e-5
GN_EPS = 1e-6
RH = 4
RDK = 256
RDV = 512
SBH = 16
SBD = 64
ROPE_BASE = 10000.0
NEG = -30000.0
DUMMY_REV = 2


class Trk:
    ENGS = ("pe", "act", "dve", "pool", "sp")

    NDMA = 48

    def __init__(self, nc, es):
        dma_sems = tuple(f"dq{i}" for i in range(self.NDMA))
        self.dmap = {}
        self.nc = nc
        self.sems = {}
        self.cnt = {}
        self.seen = {e: {} for e in self.ENGS}
        self.lastw = {}
        self.readers = {}
        self.prog = {e: [] for e in self.ENGS}
        self.nwait = 0
        self.nins = 0
        self.vcs = {}
        for name in ("pe", "act", "dve", "pool") + tuple(dma_sems):
            self.sems[name] = es.enter_context(nc.semaphore(name))
            self.cnt[name] = 0

    def op(self, eng, meth, *args, reads=(), writes=(), sem=None, inc=1, **kw):
        reads = list(reads)
        writes = list(writes)
        for k in list(reads):
            if k.startswith("ps"):
                writes.append(k)
        deps = set()
        for k in reads:
            lw = self.lastw.get(k)
            if lw is not None:
                deps.add(lw)
        for k in writes:
            lw = self.lastw.get(k)
            if lw is not None:
                deps.add(lw)
            for s, v in self.readers.get(k, {}).items():
                deps.add((s, v))
        evc = self.seen[eng]
        waits = []
        for s, v in sorted(deps, key=lambda sv: -sv[1]):
            if eng == "pe" and s == "pe":
                continue
            if evc.get(s, 0) >= v:
                continue
            waits.append((s, v))
            self.nwait += 1
            vc = self.vcs.get((s, v))
            if vc is not None:
                for k2, v2 in vc:
                    if evc.get(k2, 0) < v2:
                        evc[k2] = v2
            if evc.get(s, 0) < v:
                evc[s] = v
        self.nins += 1
        s = sem or eng
        self.cnt[s] += inc
        v = self.cnt[s]
        self.prog[eng].append((waits, meth, args, kw, s, inc))
        snap = dict(evc)
        snap[s] = v
        self.vcs[(s, v)] = tuple(snap.items())
        for k in reads:
            r = self.readers.setdefault(k, {})
            r[s] = max(r.get(s, 0), v)
        for k in writes:
            self.lastw[k] = (s, v)
            self.readers[k] = {}

    def dma(self, q, out, in_, reads=(), writes=(), sem=None, sk=None, **kw):
        if sk is None:
            sk = writes[0] if (writes and not writes[0].startswith("DR_")) else reads[0]
        if sk not in self.dmap:
            assert len(self.dmap) < self.NDMA, "out of dma semaphores"
            self.dmap[sk] = f"dq{len(self.dmap)}"
        self.op(q, "dma_start", reads=reads, writes=writes, sem=self.dmap[sk], inc=16, out=out, in_=in_, **kw)

    def barrier(self):
        for e in self.ENGS:
            waits = []
            for s, v in self.cnt.items():
                if e == "pe" and s == "pe":
                    continue
                if v > 0 and self.seen[e].get(s, 0) < v:
                    waits.append((s, v))
                    self.seen[e][s] = v
            if waits:
                self.prog[e].append((waits, None, None, None, None, 0))
        self.vcs = {}

    def flush(self, es):
        self.barrier()
        self.dmap = {}
        block = es.enter_context(self.nc.Block())
        prog = self.prog
        self.prog = {e: [] for e in self.ENGS}
        if not hasattr(self, "real_base"):
            self.real_base = {s: 0 for s in self.sems}
            self.virt_base = {s: 0 for s in self.sems}
        comp = ("pe", "act", "dve", "pool")
        targets = {s: set() for s in comp}
        for e in self.ENGS:
            for waits, meth, args, kw, s, inc in prog[e]:
                for ws, wv in waits:
                    if ws in targets:
                        targets[ws].add(wv)
        rank = {}
        for s in comp:
            tl = sorted(targets[s])
            rank[s] = {v: self.real_base[s] + i + 1 for i, v in enumerate(tl)}
            assert all(v > self.virt_base[s] for v in tl), "wait on a pre-barrier value"
        virt = dict(self.virt_base)

        def replay(engname):
            def f(e):
                for waits, meth, args, kw, s, inc in prog[engname]:
                    ww = [(ws, rank[ws][wv] if ws in rank else wv) for ws, wv in waits]
                    if meth is None:
                        for ws, wv in ww:
                            e.wait_ge(self.sems[ws], wv)
                        continue
                    for ws, wv in ww[1:]:
                        e.wait_ge(self.sems[ws], wv)
                    ins = getattr(e, meth)(*args, **kw)
                    if ww:
                        ins._wait_ge(self.sems[ww[0][0]], ww[0][1])
                    if s in rank:
                        virt[s] += inc
                        if virt[s] in rank[s]:
                            ins.then_inc(self.sems[s], 1)
                    else:
                        ins.then_inc(self.sems[s], inc)
            return f

        block.sync(replay("sp"))
        block.tensor(replay("pe"))
        block.scalar(replay("act"))
        block.vector(replay("dve"))
        block.gpsimd(replay("pool"))
        for s in comp:
            self.real_base[s] += len(targets[s])
            self.virt_base[s] = self.cnt[s]


def _consts(SEQ, PAST):
    posmax = max(SEQ, PAST + DEC_SEQ)
    half = RDK // 2
    inv = (ROPE_BASE ** (-np.arange(half, dtype=np.float32) / half)).astype(np.float32)
    ang = inv[:, None] * np.arange(posmax, dtype=np.float32)[None, :]
    cosT = np.cos(ang).astype(np.float32)
    sinT = np.sin(ang).astype(np.float32)
    log_g = np.log1p(-np.exp2(-5.0 - np.arange(RH, dtype=np.float32))).astype(np.float32)
    kk = np.arange(128)
    tri = np.where(kk[:, None] >= kk[None, :], -1.0, 0.0).astype(np.float32)
    negm_p = np.tile(np.where(kk[:, None] >= kk[None, :], NEG, 0.0).astype(np.float32), (1, 4))
    k64 = np.arange(64)
    negm_s = np.tile(np.where(k64[:, None] >= k64[None, :], NEG, 0.0).astype(np.float32), (1, 8))
    out = {"cosT": cosT, "sinT": sinT, "tri": tri, "negm_p": negm_p, "negm_s": negm_s,
           "iota128": np.tile(np.arange(128, dtype=np.float32)[None, :], (128, 1))}
    for name, L, nstream in (("p", 128, 1), ("s", 64, 2)):
        pos = np.arange(L, dtype=np.float32)
        dm = np.zeros((128, RH, 128), np.float32)
        qd = np.zeros((128, RH, 128), np.float32)
        kd = np.zeros((128, RH), np.float32)
        for s in range(nstream):
            for h in range(RH):
                diff = pos[None, :] - pos[:, None]
                blk = np.where(diff >= 0, np.exp(log_g[h] * np.maximum(diff, 0.0)), 0.0)
                dm[s * L:(s + 1) * L, h, s * L:(s + 1) * L] = blk
                qd[:, h, s * L:(s + 1) * L] = np.exp(log_g[h] * (pos + 1.0))[None, :]
                kd[s * L:(s + 1) * L, h] = np.exp(log_g[h] * (L - 1.0 - pos))
        out["dm_" + name] = dm.astype(np.float32)
        out["qd_" + name] = qd.astype(np.float32)
        out["kd_" + name] = kd.astype(np.float32)
        out["sdec_" + name] = [float(np.exp(np.float32(log_g[h] * L))) for h in range(RH)]
    return out


class Builder:
    def __init__(self, SEQ, PAST, debug=(), upto=9):
        self.upto = upto
        self.SEQ, self.PAST = SEQ, PAST
        self.NT = SEQ // 128
        self.NTILE = self.NT + 1
        self.NTOK = SEQ + 128
        self.debug = set(debug)
        self.C = _consts(SEQ, PAST)
        self.nc = bass.Bass("TRN2", target_bir_lowering=False)
        self.bank_i = 0
        self.free_banks = list(range(8))

    def din(self, name, shape, dt=F32):
        return self.nc.dram_tensor(name, list(shape), dt, kind="ExternalInput").ap()

    def dout(self, name, shape, dt=F32):
        return self.nc.dram_tensor(name, list(shape), dt, kind="ExternalOutput").ap()

    def dscr(self, name, shape, dt):
        kind = "ExternalOutput" if name in self.debug else "Internal"
        return self.nc.dram_tensor(name, list(shape), dt, kind=kind).ap()

    def nb(self):
        b = self.free_banks[self.bank_i % len(self.free_banks)]
        self.bank_i += 1
        return b

    def reserve_banks(self, n):
        r = self.free_banks[-n:]
        self.free_banks = self.free_banks[:-n]
        return r

    def release_banks(self):
        self.free_banks = list(range(8))

    def build(self):
        nc = self.nc
        NTOK, SEQ, PAST = self.NTOK, self.SEQ, self.PAST
        posmax = self.C["cosT"].shape[1]
        I = self.I = {}
        I["xin"] = self.din("xin", [NTOK, D])
        I["state_in"] = self.din("state_in", [2, RH, RDK, RDV])
        I["ret_w_in"] = self.din("ret_w_in", [D, 6144])
        I["ret_w_o"] = self.din("ret_w_o", [2048, D])
        I["ln_g"] = self.din("ln_g", [4, D])
        I["ln_b"] = self.din("ln_b", [4, D])
        I["cosT"] = self.din("cosT", [128, posmax])
        I["sinT"] = self.din("sinT", [128, posmax])
        for n in ("p", "s"):
            I["dm_" + n] = self.din("dm_" + n, [128, RH, 128])
            I["qd_" + n] = self.din("qd_" + n, [128, RH, 128])
            I["kd_" + n] = self.din("kd_" + n, [128, RH])
        I["peer_w_q"] = self.din("peer_w_q", [2, D, 2048])
        I["peer_keys_a"] = self.din("peer_keys_a", [2, 8, 128, 128])
        I["peer_keys_b"] = self.din("peer_keys_b", [2, 8, 128, 128])
        I["peer_u"] = self.din("peer_u", [2, 16384, D])
        I["peer_v"] = self.din("peer_v", [2, 16384, D])
        I["iota128"] = self.din("iota128", [128, 128])
        I["sb_w_qkv"] = self.din("sb_w_qkv", [D, 3072])
        I["sb_w_o"] = self.din("sb_w_o", [D, D])
        I["cache_k"] = self.din("cache_k", [2, SBH, PAST, SBD])
        I["cache_v"] = self.din("cache_v", [2, SBH, PAST, SBD])
        I["tri"] = self.din("tri", [128, 128])
        I["negm_p"] = self.din("negm_p", [128, 512])
        I["negm_s"] = self.din("negm_s", [64, 512])
        O = self.O = {}
        O["ret_p"] = self.dout("ret_p", [RH, RDK, RDV])
        O["ret_s"] = self.dout("ret_s", [2, RH, RDK, RDV])
        O["kp"] = self.dout("kp", [SBH, SEQ, SBD])
        O["vp"] = self.dout("vp", [SBH, SEQ, SBD])
        O["ks"] = self.dout("ks", [2, SBH, DEC_SEQ, SBD])
        O["vs"] = self.dout("vs", [2, SBH, DEC_SEQ, SBD])
        O["y"] = self.dout("y", [NTOK, D])
        S = self.S = {}
        S["Yscr"] = self.dscr("Yscr", [NTOK, 2048], BF16)
        S["X1"] = self.dscr("X1", [NTOK, D], F32)
        S["X2"] = self.dscr("X2", [NTOK, D], F32)
        S["X3"] = self.dscr("X3", [NTOK, D], F32)
        if "DBG_oT" in self.debug:
            S["DBG_oT"] = self.dscr("DBG_oT", [self.NTILE, 128, 8, 128], BF16)
            S["DBG_W"] = self.dscr("DBG_W", [4, 128, 512], F32)
        S["VS"] = self.dscr("VS", [NTOK, D], BF16)
        S["QT"] = self.dscr("QT", [8, 128, NTOK], BF16)
        S["KT"] = self.dscr("KT", [8, 128, NTOK], BF16)
        S["UT"] = self.dscr("UT", [128, 128, 8, 128], BF16)
        S["VB"] = self.dscr("VB", [128, 128, D], BF16)

        with ExitStack() as es0:
            self.T = Trk(nc, es0)
            T = self.T
            self.ps = [es0.enter_context(nc.psum_tensor(f"ps{i}", [128, 512], F32)) for i in range(8)]
            self.ident_f = es0.enter_context(nc.sbuf_tensor("ident_f", [128, 128], F32))
            self.ident_b = es0.enter_context(nc.sbuf_tensor("ident_b", [128, 128], BF16))
            T.op("pool", "memset", self.ident_f[:], 0.0, writes=["ident_f"])
            T.op("pool", "memset", self.ident_b[:], float(DUMMY_REV), writes=["ident_b"])
            T.op("pool", "affine_select", out=self.ident_f[:], in_=self.ident_f[:], pattern=[[-1, 128]],
                 compare_op=ALU.not_equal, fill=1.0, base=0, channel_multiplier=1,
                 reads=["ident_f"], writes=["ident_f"])
            T.op("pool", "tensor_copy", out=self.ident_b[:], in_=self.ident_f[:], reads=["ident_f"], writes=["ident_b"])
            with ExitStack() as es:
                self.phase_retA(es)
                T.flush(es)
            with ExitStack() as es:
                self.phase_retB(es)
                T.flush(es)
            if self.upto >= 2:
                with ExitStack() as es:
                    self.phase_peer_prep(es, 0)
                    T.flush(es)
                with ExitStack() as es:
                    self.phase_peer(es, 0, S["X1"], "DR_X1", S["X2"], "DR_X2", 1)
                    T.flush(es)
            if self.upto >= 3:
                with ExitStack() as es:
                    self.phase_qkv(es)
                    T.flush(es)
            if self.upto >= 4:
                with ExitStack() as es:
                    self.phase_att(es)
                    T.flush(es)
            if self.upto >= 5:
                with ExitStack() as es:
                    self.phase_peer_prep(es, 1)
                    T.flush(es)
                with ExitStack() as es:
                    self.phase_peer(es, 1, S["X3"], "DR_X3", O["y"], "DR_y", 3)
                    T.flush(es)
            with ExitStack() as es:
                T.flush(es)
        return nc

    def load_weight_bf16(self, es_unused, dst, dst_key, src2d, ncols, stage, kchunks, col0=0):
        T, nc = self.T, self.nc
        step = stage["cols"]
        i = 0
        for c0 in range(0, ncols, step):
            cw = min(step, ncols - c0)
            slot = stage["i"] % 2
            stage["i"] += 1
            st = stage["t"][slot]
            key = stage["keys"][slot] if "keys" in stage else f"wstage{slot}"
            T.dma("sp", st[:, 0:kchunks, 0:cw],
                  src2d[:, col0 + c0:col0 + c0 + cw].rearrange("(kc p) n -> p kc n", p=128),
                  writes=[key])
            eng = ("dve", "act")[i % 2]
            i += 1
            if eng == "dve":
                T.op("dve", "tensor_copy", out=dst[:, :, c0:c0 + cw], in_=st[:, 0:kchunks, 0:cw], reads=[key], writes=[dst_key])
            else:
                T.op("act", "copy", out=dst[:, :, c0:c0 + cw], in_=st[:, 0:kchunks, 0:cw], reads=[key], writes=[dst_key])

    def transpose_to_bf16(self, src_f32, src_key, dst, dst_key, nchunks):
        T = self.T
        for g0 in range(0, nchunks, 4):
            b = self.nb()
            pk = f"ps{b}"
            n = min(4, nchunks - g0)
            for j in range(n):
                T.op("pe", "transpose", self.ps[b][:, j * 128:(j + 1) * 128],
                     src_f32[:, (g0 + j) * 128:(g0 + j + 1) * 128], self.ident_f[:],
                     reads=[src_key, "ident_f"], writes=[pk])
            T.op("act", "copy", out=dst[:, g0:g0 + n, :],
                 in_=self.ps[b][:, 0:n * 128].rearrange("p (c n) -> p c n", c=n),
                 reads=[pk], writes=[dst_key])

    def layer_norm_store(self, tin, tin_key, gt, bt, out_t, out_key, tmp):
        T = self.T
        bst, mv, rs = tmp["bst"], tmp["mv"], tmp["rs"]
        for j in range(2):
            T.op("dve", "bn_stats", out=bst[:, j, :], in_=tin[:, j * 512:(j + 1) * 512], reads=[tin_key], writes=["ln_bst"])
        T.op("dve", "bn_aggr", out=mv[:], in_=bst[:].rearrange("p a b -> p (a b)"), reads=["ln_bst"], writes=["ln_mv"])
        T.op("act", "activation", out=rs[:, 0:1], in_=mv[:, 1:2], func=AF.Sqrt, bias=LN_EPS, scale=1.0,
             reads=["ln_mv"], writes=["ln_rs"])
        T.op("dve", "reciprocal", out=rs[:, 1:2], in_=rs[:, 0:1], reads=["ln_rs"], writes=["ln_rs"])
        T.op("dve", "tensor_scalar", out=out_t[:], in0=tin[:], scalar1=mv[:, 0:1], scalar2=rs[:, 1:2],
             op0=ALU.subtract, op1=ALU.mult, reads=[tin_key, "ln_mv", "ln_rs"], writes=[out_key])
        T.op("pool", "tensor_tensor", out=out_t[:], in0=out_t[:], in1=gt[:], op=ALU.mult, reads=[out_key, "lng"], writes=[out_key])
        T.op("pool", "tensor_tensor", out=out_t[:], in0=out_t[:], in1=bt[:], op=ALU.add, reads=[out_key, "lnb"], writes=[out_key])

    def phase_retA(self, es):
        nc, T, I, O, S = self.nc, self.T, self.I, self.O, self.S
        NT = self.NT
        sb = lambda name, shape, dt: es.enter_context(nc.sbuf_tensor("A_" + name, list(shape), dt))
        wqkv = sb("wqkv", [128, 8, 4096], BF16)
        stage = {"t": [sb("wst0", [128, 8, 256], F32), sb("wst1", [128, 8, 256], F32)], "i": 0, "cols": 256}
        self.load_weight_bf16(es, wqkv, "wqkv", I["ret_w_in"], 4096, stage, 8)
        Sf = [sb("Sf0", [128, RH, 2, RDV], F32), sb("Sf1", [128, RH, 2, RDV], F32)]
        Sb = [sb("Sb0", [128, RH, 2, RDV], BF16), sb("Sb1", [128, RH, 2, RDV], BF16)]
        xt = [sb("xt0", [128, D], F32), sb("xt1", [128, D], F32)]
        xT = [sb("xT0", [128, 8, 128], BF16), sb("xT1", [128, 8, 128], BF16)]
        cs = [sb("cs0", [128, 2, 128], F32), sb("cs1", [128, 2, 128], F32)]
        qT = sb("qT", [128, 8, 128], BF16)
        qdT = sb("qdT", [128, 8, 128], BF16)
        qdT2 = [sb("qdTa", [128, 8, 128], BF16), sb("qdTb", [128, 8, 128], BF16)]
        kT = sb("kT", [128, 8, 128], BF16)
        tA = sb("tA", [128, RH, 128], F32)
        tB = sb("tB", [128, RH, 128], F32)
        tC = sb("tC", [128, RH, 128], F32)
        tD = sb("tD", [128, RH, 128], F32)
        ktm = sb("ktm", [128, RH, 256], BF16)
        vv = sb("vv", [128, RH, RDV], BF16)
        vd = sb("vd", [128, RH, RDV], BF16)
        ATm = sb("ATm", [128, RH, 128], BF16)
        Yt = [sb("Y0", [128, 2048], BF16), sb("Y1", [128, 2048], BF16)]
        dm = {n: sb("dm_" + n, [128, RH, 128], F32) for n in "ps"}
        qd = {n: sb("qd_" + n, [128, RH, 128], F32) for n in "ps"}
        kd = {n: sb("kd_" + n, [128, RH], F32) for n in "ps"}
        bst = sb("gn_bst", [128, RH, 6], F32)
        mv = sb("gn_mv", [128, RH, 2], F32)
        rs = sb("gn_rs", [128, RH, 3], F32)
        for n in "ps":
            T.dma("sp", dm[n][:], I["dm_" + n], writes=["dm_" + n])
            T.dma("sp", qd[n][:], I["qd_" + n], writes=["qd_" + n])
            T.dma("sp", kd[n][:], I["kd_" + n], writes=["kd_" + n])
        T.op("pool", "memset", Sf[0][:], 0.0, writes=["Sf0"])
        T.op("pool", "memset", Sb[0][:], 0.0, writes=["Sb0"])
        T.op("pool", "memset", qdT2[0][:], 0.0, writes=["qdTa"])
        T.op("pool", "memset", qdT2[1][:], 0.0, writes=["qdTb"])

        for t in range(self.NTILE):
            sample = (t == NT)
            n = "s" if sample else "p"
            sl = t % 2
            r0 = t * 128
            xk, xTk, csk, Yk = f"xt{sl}", f"xT{sl}", f"cs{sl}", f"Y{sl}"
            if sample:
                T.dma("pool", O["ret_p"].rearrange("h (dc p) e -> p h dc e", p=128), Sf[0][:], reads=["Sf0"])
                for s in range(2):
                    T.dma("sp", Sf[s][:], I["state_in"][s].rearrange("h (dc p) e -> p h dc e", p=128), writes=[f"Sf{s}"])
                    T.op("pool", "tensor_copy", out=Sb[s][:], in_=Sf[s][:], reads=[f"Sf{s}"], writes=[f"Sb{s}"])
            T.dma("sp", xt[sl][:], I["xin"][r0:r0 + 128, :], writes=[xk])
            if sample:
                for s in range(2):
                    T.dma("sp", cs[sl][:, 0, s * 64:(s + 1) * 64], I["cosT"][:, self.PAST:self.PAST + 64], writes=[csk])
                    T.dma("sp", cs[sl][:, 1, s * 64:(s + 1) * 64], I["sinT"][:, self.PAST:self.PAST + 64], writes=[csk])
            else:
                T.dma("sp", cs[sl][:, 0, :], I["cosT"][:, r0:r0 + 128], writes=[csk])
                T.dma("sp", cs[sl][:, 1, :], I["sinT"][:, r0:r0 + 128], writes=[csk])
            self.transpose_to_bf16(xt[sl], xk, xT[sl], xTk, 8)
            cosb = cs[sl][:, 0:1, :].to_broadcast([128, RH, 128])
            sinb = cs[sl][:, 1:2, :].to_broadcast([128, RH, 128])
            for which, dstT, col_base, scl in (("q", qT, 0, 1.0), ("k", kT, 1024, RDK ** -0.5)):
                banks = []
                for half in range(2):
                    b = self.nb()
                    banks.append(b)
                    for h in range(RH):
                        c0 = col_base + h * 256 + half * 128
                        for kc in range(8):
                            T.op("pe", "matmul", self.ps[b][:, h * 128:(h + 1) * 128], wqkv[:, kc, c0:c0 + 128],
                                 xT[sl][:, kc, :], start=(kc == 0), stop=(kc == 7),
                                 reads=["wqkv", xTk], writes=[f"ps{b}"])
                p1 = self.ps[banks[0]][:].rearrange("p (h n) -> p h n", h=RH)
                p2 = self.ps[banks[1]][:].rearrange("p (h n) -> p h n", h=RH)
                k1, k2 = f"ps{banks[0]}", f"ps{banks[1]}"
                T.op("dve", "scalar_tensor_tensor", out=tA[:], in0=p1, scalar=scl, in1=cosb, op0=ALU.mult, op1=ALU.mult,
                     reads=[k1, csk], writes=["tA"])
                T.op("dve", "scalar_tensor_tensor", out=tC[:], in0=p1, scalar=scl, in1=sinb, op0=ALU.mult, op1=ALU.mult,
                     reads=[k1, csk], writes=["tC"])
                T.op("dve", "scalar_tensor_tensor", out=tB[:], in0=p2, scalar=scl, in1=sinb, op0=ALU.mult, op1=ALU.mult,
                     reads=[k2, csk], writes=["tB"])
                T.op("dve", "scalar_tensor_tensor", out=tD[:], in0=p2, scalar=scl, in1=cosb, op0=ALU.mult, op1=ALU.mult,
                     reads=[k2, csk], writes=["tD"])
                T.op("pool", "tensor_tensor", out=dstT[:, 0:4, :], in0=tA[:], in1=tB[:], op=ALU.subtract,
                     reads=["tA", "tB"], writes=[which + "T"])
                T.op("pool", "tensor_tensor", out=dstT[:, 4:8, :], in0=tC[:], in1=tD[:], op=ALU.add,
                     reads=["tC", "tD"], writes=[which + "T"])
            for half in range(2):
                T.op("pool", "tensor_tensor", out=qdT[:, half * 4:(half + 1) * 4, :], in0=qT[:, half * 4:(half + 1) * 4, :],
                     in1=qd[n][:], op=ALU.mult, reads=["qT", "qd_" + n], writes=["qdT"])
            if sample:
                for s in range(2):
                    T.op("pool", "tensor_copy", out=qdT2[s][:, :, s * 64:(s + 1) * 64], in_=qdT[:, :, s * 64:(s + 1) * 64],
                         reads=["qdT"], writes=["qdT" + "ab"[s]])
            for h in range(RH):
                b = self.nb()
                for kc in range(8):
                    T.op("pe", "matmul", self.ps[b][:], xT[sl][:, kc, :], wqkv[:, kc, 2048 + h * 512:2048 + (h + 1) * 512],
                         start=(kc == 0), stop=(kc == 7), reads=["wqkv", xTk], writes=[f"ps{b}"])
                T.op("act", "copy", out=vv[:, h, :], in_=self.ps[b][:], reads=[f"ps{b}"], writes=["vv"])
                T.op("act", "mul", out=vd[:, h, :], in_=self.ps[b][:], mul=kd[n][:, h:h + 1], reads=[f"ps{b}", "kd_" + n], writes=["vd"])
            b = self.nb()
            pkb = self.ps[b][:].bitcast(BF16)
            for h in range(RH):
                for half in range(2):
                    cc = h * 2 + half
                    T.op("pe", "transpose", pkb[:, cc * 128:(cc + 1) * 128], kT[:, half * 4 + h, :], self.ident_b[:],
                         reads=["kT", "ident_b"], writes=[f"ps{b}"])
            T.op("act", "copy", out=ktm[:].rearrange("p h d -> p (h d)"), in_=pkb[:, 0:1024], reads=[f"ps{b}"], writes=["ktm"])
            b = self.nb()
            for h in range(RH):
                for half in range(2):
                    T.op("pe", "matmul", self.ps[b][:, h * 128:(h + 1) * 128], kT[:, half * 4 + h, :], qT[:, half * 4 + h, :],
                         start=(half == 0), stop=(half == 1), reads=["kT", "qT"], writes=[f"ps{b}"])
            T.op("dve", "tensor_tensor", out=ATm[:], in0=self.ps[b][:].rearrange("p (h n) -> p h n", h=RH), in1=dm[n][:],
                 op=ALU.mult, reads=[f"ps{b}", "dm_" + n], writes=["ATm"])
            obanks = []
            for h in range(RH):
                b = self.nb()
                obanks.append(b)
                T.op("pe", "matmul", self.ps[b][:], ATm[:, h, :], vv[:, h, :], start=True, stop=False,
                     reads=["ATm", "vv"], writes=[f"ps{b}"])
                srcs = [(qdT2[s], "qdT" + "ab"[s], s) for s in range(2)] if sample else [(qdT, "qdT", 0)]
                nmm = len(srcs) * 2
                i = 0
                for (qsrc, qkey, s) in srcs:
                    for half in range(2):
                        i += 1
                        T.op("pe", "matmul", self.ps[b][:], qsrc[:, half * 4 + h, :], Sb[s][:, h, half, :],
                             start=False, stop=(i == nmm), reads=[qkey, f"Sb{s}"], writes=[f"ps{b}"])
            for h in range(RH):
                b = obanks[h]
                T.op("dve", "bn_stats", out=bst[:, h, :], in_=self.ps[b][:], reads=[f"ps{b}"], writes=["gn_bst"])
            for h in range(RH):
                T.op("dve", "bn_aggr", out=mv[:, h, :], in_=bst[:, h, :], reads=["gn_bst"], writes=["gn_mv"])
            T.op("act", "activation", out=rs[:, :, 0], in_=mv[:, :, 1], func=AF.Sqrt, bias=GN_EPS, scale=1.0,
                 reads=["gn_mv"], writes=["gn_rs"])
            T.op("dve", "reciprocal", out=rs[:, :, 1], in_=rs[:, :, 0], reads=["gn_rs"], writes=["gn_rs"])
            T.op("dve", "scalar_tensor_tensor", out=rs[:, :, 2], in0=mv[:, :, 0], scalar=-1.0, in1=rs[:, :, 1],
                 op0=ALU.mult, op1=ALU.mult, reads=["gn_mv", "gn_rs"], writes=["gn_rs"])
            for h in range(RH):
                b = obanks[h]
                T.op("act", "activation", out=Yt[sl][:, h * 512:(h + 1) * 512], in_=self.ps[b][:], func=AF.Identity,
                     scale=rs[:, h, 1:2], bias=rs[:, h, 2:3], reads=[f"ps{b}", "gn_rs"], writes=[Yk])
            T.dma("pool", S["Yscr"][r0:r0 + 128, :], Yt[sl][:], reads=[Yk], writes=["DR_Yscr"])
            sdec = self.C["sdec_" + n]
            for h in range(RH):
                for dc in range(2):
                    for s in range(2 if sample else 1):
                        b = self.nb()
                        rows = slice(s * 64, (s + 1) * 64) if sample else slice(0, 128)
                        T.op("pe", "matmul", self.ps[b][:], ktm[rows, h, dc * 128:(dc + 1) * 128], vd[rows, h, :],
                             start=True, stop=True, reads=["ktm", "vd"], writes=[f"ps{b}"])
                        T.op("dve", "scalar_tensor_tensor", out=Sf[s][:, h, dc, :], in0=Sf[s][:, h, dc, :], scalar=sdec[h],
                             in1=self.ps[b][:], op0=ALU.mult, op1=ALU.add, reads=[f"Sf{s}", f"ps{b}"], writes=[f"Sf{s}"])
                        if not sample:
                            T.op("pool", "tensor_copy", out=Sb[s][:, h, dc, :], in_=Sf[s][:, h, dc, :],
                                 reads=[f"Sf{s}"], writes=[f"Sb{s}"])
        for s in range(2):
            T.dma("pool", O["ret_s"][s].rearrange("h (dc p) e -> p h dc e", p=128), Sf[s][:], reads=[f"Sf{s}"])

    def phase_retB(self, es):
        nc, T, I, O, S = self.nc, self.T, self.I, self.O, self.S
        sb = lambda name, shape, dt: es.enter_context(nc.sbuf_tensor("B_" + name, list(shape), dt))
        wg = sb("wg", [128, 8, 2048], BF16)
        wo = sb("wo", [128, 16, 1024], BF16)
        stage = {"t": [sb("wst0", [128, 16, 256], F32), sb("wst1", [128, 16, 256], F32)], "i": 0, "cols": 256}
        self.load_weight_bf16(es, wg, "wg", I["ret_w_in"], 2048, stage, 8, col0=4096)
        self.load_weight_bf16(es, wo, "wo", I["ret_w_o"], 1024, stage, 16)
        gt = sb("lng", [128, D], F32)
        bt = sb("lnb", [128, D], F32)
        T.dma("sp", gt[:], I["ln_g"][0:1, :].partition_broadcast(128)[:, 0, :], writes=["lng"])
        T.dma("sp", bt[:], I["ln_b"][0:1, :].partition_broadcast(128)[:, 0, :], writes=["lnb"])
        xt = [sb("xt0", [128, D], F32), sb("xt1", [128, D], F32)]
        xT = [sb("xT0", [128, 8, 128], BF16), sb("xT1", [128, 8, 128], BF16)]
        Yt = [sb("Y0", [128, 2048], BF16), sb("Y1", [128, 2048], BF16)]
        sg = sb("sg", [128, 2048], BF16)
        gy = sb("gy", [128, 2048], BF16)
        gyT = sb("gyT", [128, 16, 128], BF16)
        tin = sb("tin", [128, D], F32)
        xo = [sb("xo0", [128, D], F32), sb("xo1", [128, D], F32)]
        tmp = {"bst": sb("ln_bst", [128, 2, 6], F32), "mv": sb("ln_mv", [128, 2], F32), "rs": sb("ln_rs", [128, 2], F32)}
        for t in range(self.NTILE):
            sl = t % 2
            r0 = t * 128
            xk, xTk, Yk, xok = f"xt{sl}", f"xT{sl}", f"Y{sl}", f"xo{sl}"
            T.dma("sp", xt[sl][:], I["xin"][r0:r0 + 128, :], writes=[xk])
            T.dma("sp", Yt[sl][:], S["Yscr"][r0:r0 + 128, :], reads=["DR_Yscr"], writes=[Yk])
            self.transpose_to_bf16(xt[sl], xk, xT[sl], xTk, 8)
            for j in range(4):
                b = self.nb()
                for kc in range(8):
                    T.op("pe", "matmul", self.ps[b][:], xT[sl][:, kc, :], wg[:, kc, j * 512:(j + 1) * 512],
                         start=(kc == 0), stop=(kc == 7), reads=["wg", xTk], writes=[f"ps{b}"])
                T.op("act", "activation", out=sg[:, j * 512:(j + 1) * 512], in_=self.ps[b][:], func=AF.Silu,
                     reads=[f"ps{b}"], writes=["sg"])
            T.op("pool", "tensor_tensor", out=gy[:], in0=sg[:], in1=Yt[sl][:], op=ALU.mult, reads=["sg", Yk], writes=["gy"])
            for g0 in range(0, 16, 8):
                b = self.nb()
                pkb = self.ps[b][:].bitcast(BF16)
                for j in range(8):
                    T.op("pe", "transpose", pkb[:, j * 128:(j + 1) * 128], gy[:, (g0 + j) * 128:(g0 + j + 1) * 128],
                         self.ident_b[:], reads=["gy", "ident_b"], writes=[f"ps{b}"])
                T.op("act", "copy", out=gyT[:, g0:g0 + 8, :], in_=pkb[:, 0:1024].rearrange("p (c n) -> p c n", c=8),
                     reads=[f"ps{b}"], writes=["gyT"])
            for j in range(2):
                b = self.nb()
                for kc in range(16):
                    T.op("pe", "matmul", self.ps[b][:], gyT[:, kc, :], wo[:, kc, j * 512:(j + 1) * 512],
                         start=(kc == 0), stop=(kc == 15), reads=["wo", "gyT"], writes=[f"ps{b}"])
                T.op("dve", "scalar_tensor_tensor", out=tin[:, j * 512:(j + 1) * 512], in0=xt[sl][:, j * 512:(j + 1) * 512],
                     scalar=ALPHA, in1=self.ps[b][:], op0=ALU.mult, op1=ALU.add, reads=[xk, f"ps{b}"], writes=["tin"])
            self.layer_norm_store(tin, "tin", gt, bt, xo[sl], xok, tmp)
            T.dma("pool", S["X1"][r0:r0 + 128, :], xo[sl][:], reads=[xok], writes=["DR_X1"])

    def phase_peer_prep(self, es, li):
        nc, T, I, S = self.nc, self.T, self.I, self.S
        sb = lambda name, shape, dt: es.enter_context(nc.sbuf_tensor(f"PP{li}_" + name, list(shape), dt))
        uf = [sb("uf0", [128, D], F32), sb("uf1", [128, D], F32)]
        vf = [sb("vf0", [128, D], F32), sb("vf1", [128, D], F32)]
        ub = [sb("ub0", [128, 8, 128], BF16), sb("ub1", [128, 8, 128], BF16)]
        vb = [sb("vb0", [128, D], BF16), sb("vb1", [128, D], BF16)]
        for a in range(128):
            sl = a % 2
            T.dma("sp", uf[sl][:], I["peer_u"][li, a * 128:(a + 1) * 128, :], writes=[f"uf{sl}"])
            T.dma("sp", vf[sl][:], I["peer_v"][li, a * 128:(a + 1) * 128, :], writes=[f"vf{sl}"])
            self.transpose_to_bf16(uf[sl], f"uf{sl}", ub[sl], f"ub{sl}", 8)
            T.op("dve", "tensor_copy", out=vb[sl][:], in_=vf[sl][:], reads=[f"vf{sl}"], writes=[f"vb{sl}"])
            T.dma("pool", S["UT"][a], ub[sl][:], reads=[f"ub{sl}"], writes=["DR_UT"])
            T.dma("pool", S["VB"][a], vb[sl][:], reads=[f"vb{sl}"], writes=["DR_VB"])

    def phase_peer(self, es, li, xin_ap, xin_key, xout_ap, xout_key, ln_i):
        nc, T, I, S = self.nc, self.T, self.I, self.S
        sb = lambda name, shape, dt: es.enter_context(nc.sbuf_tensor(f"PM{li}_" + name, list(shape), dt))
        big = [sb(f"big{i}", [128, 2048], F32) for i in range(3)]
        wq = sb("wq", [128, 8, 2048], BF16)
        stage = {"t": [big[0][:].rearrange("p (a b) -> p a b", a=8), big[1][:].rearrange("p (a b) -> p a b", a=8)],
                 "i": 0, "cols": 256, "keys": ["big0", "big1"]}
        self.load_weight_bf16(es, wq, "wq", I["peer_w_q"][li], 2048, stage, 8)
        keysT = sb("keysT", [128, 16, 128], BF16)
        kf = big[2][:].rearrange("p (c d) -> p c d", c=16)
        for h in range(8):
            T.dma("sp", kf[:, 2 * h, :], I["peer_keys_a"][li, h], writes=["big2"])
            T.dma("sp", kf[:, 2 * h + 1, :], I["peer_keys_b"][li, h], writes=["big2"])
        self.transpose_to_bf16(big[2], "big2", keysT, "keysT", 16)
        gt = sb("lng", [128, D], F32)
        bt = sb("lnb", [128, D], F32)
        T.dma("sp", gt[:], I["ln_g"][ln_i:ln_i + 1, :].partition_broadcast(128)[:, 0, :], writes=["lng"])
        T.dma("sp", bt[:], I["ln_b"][ln_i:ln_i + 1, :].partition_broadcast(128)[:, 0, :], writes=["lnb"])
        iota = sb("iota", [128, 128], F32)
        T.dma("sp", iota[:], I["iota128"], writes=["iota"])
        iota16x = sb("iota16x", [128, 16], F32)
        T.op("dve", "tensor_scalar", out=iota16x[:], in0=iota[:, 0:16], scalar1=16.0, scalar2=None, op0=ALU.mult,
             reads=["iota"], writes=["iota16x"])
        GT = sb("GT", [128, 128, 256], BF16)
        xTs = [sb("xTs0", [128, 8, 256], BF16), sb("xTs1", [128, 8, 256], BF16)]
        xt = [[sb(f"xt{p}{ti}", [128, D], F32) for ti in range(2)] for p in range(2)]
        xTt = sb("xTt", [128, 8, 128], BF16)
        qT = sb("qT", [128, 16, 128], BF16)
        v16 = sb("v16", [128, 16, 16], F32)
        i16 = sb("i16", [128, 16, 16], U32)
        i16f = sb("i16f", [128, 16, 16], F32)
        s16 = sb("s16", [128, 8, 16], F32)
        pos = sb("pos", [128, 8, 16], U32)
        posf = sb("posf", [128, 8, 16], F32)
        jbu = sb("jbu", [128, 8, 16], U32)
        jbf = sb("jbf", [128, 8, 16], F32)
        jaf = sb("jaf", [128, 8, 16], F32)
        sel = sb("sel", [128, 3, 128], F32)
        selT = [[sb(f"selT{p}{ti}", [128, 3, 128], F32) for ti in range(2)] for p in range(2)]
        zz = sb("zz", [128, 8, 2], F32)
        OA = sb("OA", [128, 16, 128], BF16)
        OB = sb("OB", [128, 16, 128], BF16)
        utb = [sb(f"ut{i}", [128, 8, 128], BF16) for i in range(3)]
        vtb = [sb(f"vt{i}", [128, D], BF16) for i in range(3)]
        hg = [sb(f"hg{i}", [128, 256], BF16) for i in range(2)]
        GH = [sb(f"GH{i}", [128, 256], BF16) for i in range(2)]
        tin = sb("tin", [128, D], F32)
        xo = [sb("xo0", [128, D], F32)]
        tmp = {"bst": sb("ln_bst", [128, 2, 6], F32), "mv": sb("ln_mv", [128, 2], F32), "rs": sb("ln_rs", [128, 2], F32)}
        sc, work, cand = big
        cwork = work
        NTILE = self.NTILE
        nsb = (NTILE + 1) // 2
        def routing(sbi):
            p = sbi % 2
            tiles = [t for t in (2 * sbi, 2 * sbi + 1) if t < NTILE]
            for ti, t in enumerate(tiles):
                r0 = t * 128
                xk = f"xt{p}{ti}"
                T.dma("sp", xt[p][ti][:], xin_ap[r0:r0 + 128, :], reads=[xin_key], writes=[xk])
                self.transpose_to_bf16(xt[p][ti], xk, xTt, "xTt", 8)
                T.op("pool", "tensor_copy", out=xTs[p][:, :, ti * 128:(ti + 1) * 128], in_=xTt[:], reads=["xTt"], writes=[f"xTs{p}"])
                for g in range(4):
                    b = self.nb()
                    for j in range(4):
                        c = g * 4 + j
                        for kc in range(8):
                            T.op("pe", "matmul", self.ps[b][:, j * 128:(j + 1) * 128], wq[:, kc, c * 128:(c + 1) * 128],
                                 xTt[:, kc, :], start=(kc == 0), stop=(kc == 7), reads=["wq", "xTt"], writes=[f"ps{b}"])
                    T.op("act", "copy", out=qT[:, g * 4:(g + 1) * 4, :], in_=self.ps[b][:].rearrange("p (c n) -> p c n", c=4),
                         reads=[f"ps{b}"], writes=["qT"])
                    yield
                for g in range(4):
                    b = self.nb()
                    for j in range(4):
                        c = g * 4 + j
                        T.op("pe", "matmul", self.ps[b][:, j * 128:(j + 1) * 128], qT[:, c, :], keysT[:, c, :],
                             start=True, stop=True, reads=["qT", "keysT"], writes=[f"ps{b}"])
                    T.op("act", "copy", out=sc[:, g * 512:(g + 1) * 512], in_=self.ps[b][:], reads=[f"ps{b}"], writes=["big0"])
                    yield
                for c in range(16):
                    scc = sc[:, c * 128:(c + 1) * 128]
                    wkc = work[:, c * 128:(c + 1) * 128]
                    T.op("dve", "max", out=v16[:, c, 0:8], in_=scc, reads=["big0"], writes=["v16"])
                    T.op("dve", "match_replace", out=wkc, in_to_replace=v16[:, c, 0:8], in_values=scc, imm_value=-1e30,
                         reads=["big0", "v16"], writes=["big1"])
                    T.op("dve", "max", out=v16[:, c, 8:16], in_=wkc, reads=["big1"], writes=["v16"])
                    T.op("dve", "max_index", out=i16[:, c, 0:8], in_max=v16[:, c, 0:8], in_values=scc,
                         reads=["big0", "v16"], writes=["i16"])
                    T.op("dve", "max_index", out=i16[:, c, 8:16], in_max=v16[:, c, 8:16], in_values=scc,
                         reads=["big0", "v16"], writes=["i16"])
                    yield
                T.op("dve", "tensor_copy", out=i16f[:], in_=i16[:], reads=["i16"], writes=["i16f"])
                v4 = v16[:].rearrange("p (h two) k -> p h two k", two=2)
                i4 = i16f[:].rearrange("p (h two) k -> p h two k", two=2)
                cand4 = cand[:].rearrange("p (h i j) -> p h i j", h=8, i=16)
                T.op("dve", "tensor_tensor", out=cand4, in0=v4[:, :, 0, :].unsqueeze(3).to_broadcast([128, 8, 16, 16]),
                     in1=v4[:, :, 1, :].unsqueeze(2).to_broadcast([128, 8, 16, 16]), op=ALU.add,
                     reads=["v16"], writes=["big2"])
                for h in range(8):
                    cc = cand[:, h * 256:(h + 1) * 256]
                    cw = cwork[:, h * 256:(h + 1) * 256]
                    T.op("dve", "max", out=s16[:, h, 0:8], in_=cc, reads=["big2"], writes=["s16"])
                    T.op("dve", "match_replace", out=cw, in_to_replace=s16[:, h, 0:8], in_values=cc, imm_value=-1e30,
                         reads=["big2", "s16"], writes=["big1"])
                    T.op("dve", "max", out=s16[:, h, 8:16], in_=cw, reads=["big1"], writes=["s16"])
                    T.op("dve", "max_index", out=pos[:, h, 0:8], in_max=s16[:, h, 0:8], in_values=cc,
                         reads=["big2", "s16"], writes=["pos"])
                    T.op("dve", "max_index", out=pos[:, h, 8:16], in_max=s16[:, h, 8:16], in_values=cc,
                         reads=["big2", "s16"], writes=["pos"])
                    yield
                T.op("dve", "tensor_copy", out=posf[:], in_=pos[:], reads=["pos"], writes=["posf"])
                T.op("dve", "tensor_single_scalar", out=jbu[:], in_=pos[:], scalar=15, op=ALU.bitwise_and, reads=["pos"], writes=["jbu"])
                T.op("dve", "tensor_copy", out=jbf[:], in_=jbu[:], reads=["jbu"], writes=["jbf"])
                T.op("dve", "tensor_tensor", out=jaf[:], in0=posf[:], in1=jbf[:], op=ALU.subtract, reads=["posf", "jbf"], writes=["jaf"])
                E4 = sc[:].rearrange("p (h k i) -> p h k i", h=8, k=16)
                for w_, (jsrc, jkey, iot) in enumerate(((jaf, "jaf", iota16x), (jbf, "jbf", iota))):
                    T.op("dve", "tensor_tensor", out=E4,
                         in0=iot[:, 0:16].unsqueeze(1).unsqueeze(1).to_broadcast([128, 8, 16, 16]),
                         in1=jsrc[:].unsqueeze(3).to_broadcast([128, 8, 16, 16]), op=ALU.is_equal,
                         reads=["iota", "iota16x", jkey], writes=["big0"])
                    T.op("dve", "tensor_tensor", out=E4, in0=E4,
                         in1=i4[:, :, w_, :].unsqueeze(2).to_broadcast([128, 8, 16, 16]), op=ALU.mult,
                         reads=["big0", "i16f"], writes=["big0"])
                    T.op("dve", "tensor_reduce", out=sel[:, w_, :], in_=sc[:].rearrange("p (hk i) -> p hk i", i=16),
                         axis=AX.X, op=ALU.add, reads=["big0"], writes=["sel"])
                selg = sel[:, 2, :].rearrange("p (h k) -> p h k", h=8)
                T.op("dve", "tensor_tensor", out=selg, in0=s16[:], in1=s16[:, :, 0:1].to_broadcast([128, 8, 16]), op=ALU.subtract,
                     reads=["s16"], writes=["sel"])
                T.op("act", "activation", out=selg, in_=selg, func=AF.Exp, reads=["sel"], writes=["sel"])
                T.op("dve", "tensor_reduce", out=zz[:, :, 0], in_=selg, axis=AX.X, op=ALU.add, reads=["sel"], writes=["zz"])
                T.op("dve", "reciprocal", out=zz[:, :, 1], in_=zz[:, :, 0], reads=["zz"], writes=["zz"])
                T.op("dve", "tensor_tensor", out=selg, in0=selg, in1=zz[:, :, 1:2].to_broadcast([128, 8, 16]), op=ALU.mult,
                     reads=["sel", "zz"], writes=["sel"])
                b = self.nb()
                for w_ in range(3):
                    T.op("pe", "transpose", self.ps[b][:, w_ * 128:(w_ + 1) * 128], sel[:, w_, :], self.ident_f[:],
                         reads=["sel", "ident_f"], writes=[f"ps{b}"])
                T.op("act", "copy", out=selT[p][ti][:].rearrange("p w n -> p (w n)"), in_=self.ps[b][:, 0:384], reads=[f"ps{b}"], writes=[f"selT{p}{ti}"])
                yield

        def gbuild(sbi):
            p = sbi % 2
            tiles = [t for t in (2 * sbi, 2 * sbi + 1) if t < NTILE]
            for ti, t in enumerate(tiles):
                OAf = cand[:].rearrange("p (t a) -> p t a", t=16)
                for t0 in range(0, 128, 16):
                    iob = iota[:].unsqueeze(1).to_broadcast([128, 16, 128])
                    T.op("dve", "tensor_tensor", out=OAf, in0=iob,
                         in1=selT[p][ti][:, 0, t0:t0 + 16].unsqueeze(2).to_broadcast([128, 16, 128]), op=ALU.is_equal,
                         reads=["iota", f"selT{p}{ti}"], writes=["big2"])
                    T.op("dve", "tensor_tensor", out=OA[:], in0=OAf,
                         in1=selT[p][ti][:, 2, t0:t0 + 16].unsqueeze(2).to_broadcast([128, 16, 128]), op=ALU.mult,
                         reads=["big2", f"selT{p}{ti}"], writes=["OA"])
                    T.op("dve", "tensor_tensor", out=OB[:], in0=iob,
                         in1=selT[p][ti][:, 1, t0:t0 + 16].unsqueeze(2).to_broadcast([128, 16, 128]), op=ALU.is_equal,
                         reads=["iota", f"selT{p}{ti}"], writes=["OB"])
                    for g in range(4):
                        b = self.nb()
                        for j in range(4):
                            tt = g * 4 + j
                            T.op("pe", "matmul", self.ps[b][:, j * 128:(j + 1) * 128], OB[:, tt, :], OA[:, tt, :],
                                 start=True, stop=True, reads=["OA", "OB"], writes=[f"ps{b}"])
                        tk0 = ti * 128 + t0 + g * 4
                        T.op("act", "copy", out=GT[:, :, tk0:tk0 + 4], in_=self.ps[b][:].rearrange("p (t a) -> p a t", t=4),
                             reads=[f"ps{b}"], writes=["GT"])

        def stageb(sbi, gen):
            p = sbi % 2
            tiles = [t for t in (2 * sbi, 2 * sbi + 1) if t < NTILE]
            ntok = 128 * len(tiles)
            acc = self.reserve_banks(4)
            nt_ = len(tiles)

            def load(a):
                i = a % 3
                T.dma("sp", utb[i][:], S["UT"][a], reads=["DR_UT"], writes=[f"ut{i}"])
                T.dma("sp", vtb[i][:], S["VB"][a], reads=["DR_VB"], writes=[f"vt{i}"])

            def hmm(a):
                b = self.nb()
                i = a % 3
                for dc in range(8):
                    T.op("pe", "matmul", self.ps[b][:, 0:ntok], utb[i][:, dc, :], xTs[p][:, dc, 0:ntok],
                         start=(dc == 0), stop=(dc == 7), reads=[f"ut{i}", f"xTs{p}"], writes=[f"ps{b}"])
                return b

            load(0)
            load(1)
            hbs = {0: hmm(0)}
            for a in range(128):
                if a + 2 < 128:
                    load(a + 2)
                if a + 1 < 128:
                    hbs[a + 1] = hmm(a + 1)
                hb = hbs.pop(a)
                j = a % 2
                T.op("act", "activation", out=hg[j][:, 0:ntok], in_=self.ps[hb][:, 0:ntok], func=AF.Gelu,
                     reads=[f"ps{hb}"], writes=[f"hg{j}"])
                T.op("dve", "tensor_tensor", out=GH[j][:, 0:ntok], in0=hg[j][:, 0:ntok], in1=GT[:, a, 0:ntok], op=ALU.mult,
                     reads=[f"hg{j}", "GT"], writes=[f"GH{j}"])
                i = a % 3
                for ti in range(nt_):
                    for half in range(2):
                        b = acc[ti * 2 + half]
                        T.op("pe", "matmul", self.ps[b][:], GH[j][:, ti * 128:(ti + 1) * 128], vtb[i][:, half * 512:(half + 1) * 512],
                             start=(a == 0), stop=(a == 127), reads=[f"GH{j}", f"vt{i}"], writes=[f"ps{b}"])
                if gen is not None and a % 2 == 0:
                    if next(gen, "done") == "done":
                        gen = None
            for ti, t in enumerate(tiles):
                r0 = t * 128
                for half in range(2):
                    b = acc[ti * 2 + half]
                    T.op("dve", "scalar_tensor_tensor", out=tin[:, half * 512:(half + 1) * 512],
                         in0=xt[p][ti][:, half * 512:(half + 1) * 512], scalar=ALPHA, in1=self.ps[b][:],
                         op0=ALU.mult, op1=ALU.add, reads=[f"xt{p}{ti}", f"ps{b}"], writes=["tin"])
                self.layer_norm_store(tin, "tin", gt, bt, xo[0], "xo0", tmp)
                T.dma("pool", xout_ap[r0:r0 + 128, :], xo[0][:], reads=["xo0"], writes=[xout_key])
            if gen is not None:
                for _ in gen:
                    pass
            self.release_banks()

        for _ in routing(0):
            pass
        for sbi in range(nsb):
            gbuild(sbi)
            gen = routing(sbi + 1) if sbi + 1 < nsb else None
            stageb(sbi, gen)

    def phase_qkv(self, es):
        nc, T, I, O, S = self.nc, self.T, self.I, self.O, self.S
        NT, SEQ = self.NT, self.SEQ
        sb = lambda name, shape, dt: es.enter_context(nc.sbuf_tensor("Q_" + name, list(shape), dt))
        wqkv = sb("wqkv", [128, 8, 3072], BF16)
        stage = {"t": [sb("wst0", [128, 8, 256], F32), sb("wst1", [128, 8, 256], F32)], "i": 0, "cols": 256}
        self.load_weight_bf16(es, wqkv, "wqkv", I["sb_w_qkv"], 3072, stage, 8)
        xt = [sb("xt0", [128, D], F32), sb("xt1", [128, D], F32)]
        xT = [sb("xT0", [128, 8, 128], BF16), sb("xT1", [128, 8, 128], BF16)]
        ko = [sb("ko0", [128, D], F32), sb("ko1", [128, D], F32)]
        vo = [sb("vo0", [128, D], F32), sb("vo1", [128, D], F32)]
        vbf = [sb("vbf0", [128, D], BF16), sb("vbf1", [128, D], BF16)]
        qTt = [sb("qTt0", [128, 8, 128], BF16), sb("qTt1", [128, 8, 128], BF16)]
        kTt = [sb("kTt0", [128, 8, 128], BF16), sb("kTt1", [128, 8, 128], BF16)]
        for t in range(self.NTILE):
            sl = t % 2
            r0 = t * 128
            xk, xTk = f"xt{sl}", f"xT{sl}"
            T.dma("sp", xt[sl][:], S["X2"][r0:r0 + 128, :], reads=["DR_X2"], writes=[xk])
            self.transpose_to_bf16(xt[sl], xk, xT[sl], xTk, 8)
            for which, dst, col_base in (("k", ko[sl], 1024), ("v", vo[sl], 2048)):
                for j in range(2):
                    b = self.nb()
                    for kc in range(8):
                        T.op("pe", "matmul", self.ps[b][:], xT[sl][:, kc, :], wqkv[:, kc, col_base + j * 512:col_base + (j + 1) * 512],
                             start=(kc == 0), stop=(kc == 7), reads=["wqkv", xTk], writes=[f"ps{b}"])
                    T.op("act", "copy", out=dst[:, j * 512:(j + 1) * 512], in_=self.ps[b][:], reads=[f"ps{b}"], writes=[f"{which}o{sl}"])
            T.op("pool", "tensor_copy", out=vbf[sl][:], in_=vo[sl][:], reads=[f"vo{sl}"], writes=[f"vbf{sl}"])
            T.dma("pool", S["VS"][r0:r0 + 128, :], vbf[sl][:], reads=[f"vbf{sl}"], writes=["DR_VS"])
            for which, src in (("k", ko[sl]), ("v", vo[sl])):
                if t < NT:
                    T.dma("pool", O[which + "p"][:, r0:r0 + 128, :].rearrange("h s d -> s h d"),
                          src[:].rearrange("p (h d) -> p h d", h=SBH), reads=[f"{which}o{sl}"])
                else:
                    for s_ in range(2):
                        T.dma("pool", O[which + "s"][s_].rearrange("h s d -> s h d"),
                              src[s_ * 64:(s_ + 1) * 64, :].rearrange("p (h d) -> p h d", h=SBH), reads=[f"{which}o{sl}"])
            for which, dst, col_base, scl, scr in (("q", qTt[sl], 0, SBD ** -0.5, "QT"), ("k", kTt[sl], 1024, 1.0, "KT")):
                for g in range(2):
                    b = self.nb()
                    for j in range(4):
                        c = g * 4 + j
                        for kc in range(8):
                            T.op("pe", "matmul", self.ps[b][:, j * 128:(j + 1) * 128], wqkv[:, kc, col_base + c * 128:col_base + (c + 1) * 128],
                                 xT[sl][:, kc, :], start=(kc == 0), stop=(kc == 7), reads=["wqkv", xTk], writes=[f"ps{b}"])
                    T.op("act", "mul", out=dst[:, g * 4:(g + 1) * 4, :], in_=self.ps[b][:].rearrange("p (c n) -> p c n", c=4),
                         mul=scl, reads=[f"ps{b}"], writes=[f"{which}Tt{sl}"])
                T.dma("pool", S[scr][:, :, r0:r0 + 128].rearrange("c p t -> p c t"), dst[:], reads=[f"{which}Tt{sl}"], writes=["DR_" + scr])

    def sb_stage1(self, blk, st):
        T = self.T
        nk = blk["nk"]
        a = self.nb()
        blk["a"] = a
        i = st["i"]
        for (lh, rh, c0, ncol, keys) in blk["zmms"]:
            T.op("pe", "matmul", self.ps[a][0:nk, c0:c0 + ncol], lh, rh, start=True, stop=(blk["negm"] is None),
                 reads=keys, writes=[f"ps{a}"])
            if blk["negm"] is not None:
                T.op("pe", "matmul", self.ps[a][0:nk, c0:c0 + ncol], self.ident_b[0:nk, 0:nk], blk["negm"][:, c0:c0 + ncol],
                     start=False, stop=True, reads=["ident_b", "negm"], writes=[f"ps{a}"])
        e, spt = st["e"][i % 2], st["sp"][i % 3]
        ek, spk = f"sbe{i % 2}", f"sbsp{i % 3}"
        T.op("act", "activation", out=e[0:nk, :], in_=self.ps[a][0:nk, :], func=AF.Exp, reads=[f"ps{a}"], writes=[ek])
        T.op("act", "activation", out=spt[0:nk, :], in_=e[0:nk, :], func=AF.Ln, bias=1.0, scale=1.0, reads=[ek], writes=[spk])
        blk["spt"], blk["spk"] = spt, spk
        nxt = (i + 1) % 3
        if i == 0:
            if nk < 128:
                T.op("pool", "memset", st["SPf"][:], 0.0, writes=["sbSPf"])
            T.op("dve", "tensor_copy", out=st["SPf"][0:nk, :], in_=spt[0:nk, :], reads=[spk], writes=["sbSPf"])
        else:
            T.op("dve", "tensor_tensor", out=st["SPf"][0:nk, :], in0=st["SPf"][0:nk, :], in1=spt[0:nk, :], op=ALU.add,
                 reads=["sbSPf", spk], writes=["sbSPf"])
        T.op("dve", "tensor_copy", out=st["SPb"][nxt][:], in_=st["SPf"][:], reads=["sbSPf"], writes=[f"sbSPb{nxt}"])
        blk["i"] = i
        st["i"] = i + 1

    def sb_stage2(self, blk, st, nblocks):
        T = self.T
        nk, i = blk["nk"], blk["i"]
        b = self.nb()
        for (lh, rh, c0, ncol, keys) in blk["zmms"]:
            T.op("pe", "matmul", self.ps[b][0:nk, c0:c0 + ncol], lh, rh, start=True, stop=False, reads=keys, writes=[f"ps{b}"])
            if blk["negm"] is not None:
                T.op("pe", "matmul", self.ps[b][0:nk, c0:c0 + ncol], self.ident_b[0:nk, 0:nk], blk["negm"][:, c0:c0 + ncol],
                     start=False, stop=False, reads=["ident_b", "negm"], writes=[f"ps{b}"])
            T.op("pe", "matmul", self.ps[b][0:nk, c0:c0 + ncol], self.tri[0:nk, 0:nk], blk["spt"][0:nk, c0:c0 + ncol],
                 start=False, stop=(i == 0), reads=["sbtri", blk["spk"]], writes=[f"ps{b}"])
            if i > 0:
                T.op("pe", "matmul", self.ps[b][0:nk, c0:c0 + ncol], self.negones[:, 0:nk], st["SPb"][i % 3][:, c0:c0 + ncol],
                     start=False, stop=True, reads=["sbones", f"sbSPb{i % 3}"], writes=[f"ps{b}"])
        W = st["W"][i % 2]
        wk = f"sbW{i % 2}"
        T.op("act", "activation", out=W[0:nk, :], in_=self.ps[b][0:nk, :], func=AF.Exp, reads=[f"ps{b}"], writes=[wk])
        if "DBG_oT" in self.debug and not getattr(self, "_dbgW", False):
            self._dbgW = True
            dd = st["e"][(i + 1) % 2]
            dk = f"sbe{(i + 1) % 2}"
            T.dma("pool", self.S["DBG_W"][0], st["e"][i % 2][:], reads=[f"sbe{i % 2}"], writes=["DR_dbg"])
            T.op("dve", "tensor_copy", out=dd[:], in_=blk["spt"][:], reads=[blk["spk"]], writes=[dk])
            T.dma("pool", self.S["DBG_W"][1], dd[:], reads=[dk], writes=["DR_dbg"])
            T.op("dve", "tensor_copy", out=dd[:], in_=W[:], reads=[wk], writes=[dk])
            T.dma("pool", self.S["DBG_W"][2], dd[:], reads=[dk], writes=["DR_dbg"])
            T.op("dve", "tensor_copy", out=dd[:], in_=self.ps[b][:], reads=[f"ps{b}"], writes=[dk])
            T.dma("pool", self.S["DBG_W"][3], dd[:], reads=[dk], writes=["DR_dbg"])
        ob = st["obank"]
        if i == 0:
            T.op("pe", "matmul", self.ps[ob][:], self.ident_b[:], self.zeros_b[:], start=True, stop=False,
                 reads=["ident_b", "zeros_b"], writes=[f"ps{ob}"])
        for (lh, c0, ncol, oc0, keys) in blk["avmms"]:
            T.op("pe", "matmul", self.ps[ob][:, oc0:oc0 + ncol], lh, W[0:nk, c0:c0 + ncol], start=False, stop=(i == nblocks - 1),
                 reads=keys + [wk], writes=[f"ps{ob}"])

    def sb_run(self, blocks, st):
        st["i"] = 0
        n = len(blocks)

        def s1(blk):
            if blk.get("pre") is not None:
                blk["pre"]()
            self.sb_stage1(blk, st)

        s1(blocks[0])
        for i in range(n):
            if i + 1 < n:
                s1(blocks[i + 1])
            self.sb_stage2(blocks[i], st, n)

    def phase_att(self, es):
        nc, T, I, O, S = self.nc, self.T, self.I, self.O, self.S
        NT, SEQ, PAST = self.NT, self.SEQ, self.PAST
        sb = lambda name, shape, dt: es.enter_context(nc.sbuf_tensor("AT_" + name, list(shape), dt))
        wo = sb("wo", [128, 8, 1024], BF16)
        stage = {"t": [sb("wst0", [128, 8, 256], F32), sb("wst1", [128, 8, 256], F32)], "i": 0, "cols": 256}
        self.load_weight_bf16(es, wo, "wo", I["sb_w_o"], 1024, stage, 8)
        gt = sb("lng", [128, D], F32)
        bt = sb("lnb", [128, D], F32)
        T.dma("sp", gt[:], I["ln_g"][2:3, :].partition_broadcast(128)[:, 0, :], writes=["lng"])
        T.dma("sp", bt[:], I["ln_b"][2:3, :].partition_broadcast(128)[:, 0, :], writes=["lnb"])
        cf = stage["t"][0][:].rearrange("p a b -> p (a b)")
        self.tri = sb("tri", [128, 128], BF16)
        self.negones = sb("negones", [128, 128], BF16)
        negm_p = sb("negm_p", [128, 512], BF16)
        negm_s = sb("negm_s", [64, 512], BF16)
        T.dma("sp", cf[:, 0:128], I["tri"], writes=["wstage0"])
        T.dma("sp", cf[:, 128:640], I["negm_p"], writes=["wstage0"])
        T.dma("sp", cf[0:64, 640:1152], I["negm_s"], writes=["wstage0"])
        T.op("dve", "tensor_copy", out=self.tri[:], in_=cf[:, 0:128], reads=["wstage0"], writes=["sbtri"])
        T.op("dve", "tensor_copy", out=negm_p[:], in_=cf[:, 128:640], reads=["wstage0"], writes=["negm"])
        T.op("dve", "tensor_copy", out=negm_s[:], in_=cf[0:64, 640:1152], reads=["wstage0"], writes=["negm"])
        T.op("pool", "memset", self.negones[:], -1.0, writes=["sbones"])
        self.zeros_b = sb("zeros_b", [128, 512], BF16)
        T.op("pool", "memset", self.zeros_b[:], 0.0, writes=["zeros_b"])
        st = {"e": [sb("e0", [128, 512], F32), sb("e1", [128, 512], F32)],
              "sp": [sb(f"sp{i}", [128, 512], BF16) for i in range(3)],
              "W": [sb("W0", [128, 512], BF16), sb("W1", [128, 512], BF16)],
              "SPf": sb("SPf", [128, 512], F32),
              "SPb": [sb(f"SPb{i}", [128, 512], BF16) for i in range(3)]}
        KB = 8
        ktb = [sb(f"ktb{i}", [128, 2, KB * 128], BF16) for i in range(3)]
        vtb = [sb(f"vtb{i}", [128, KB, 256], BF16) for i in range(3)]
        qt = [sb("qt0", [128, 8, 128], BF16), sb("qt1", [128, 8, 128], BF16)]
        QZ = sb("QZ", [128, 8, 256], BF16)
        QZs = sb("QZs", [128, 8, 2, 128], BF16)
        T.op("pool", "memset", QZ[:], 0.0, writes=["QZ"])
        T.op("pool", "memset", QZs[:], 0.0, writes=["QZs"])
        oT = sb("oT", [128, 8, 128], BF16)
        xt = [sb("xt0", [128, D], F32), sb("xt1", [128, D], F32)]
        tin = sb("tin", [128, D], F32)
        xo = [sb("xo0", [128, D], F32), sb("xo1", [128, D], F32)]
        tmp = {"bst": sb("ln_bst", [128, 2, 6], F32), "mv": sb("ln_mv", [128, 2], F32), "rs": sb("ln_rs", [128, 2], F32)}
        ckf = [sb("ckf0", [128, 8, 64], F32), sb("ckf1", [128, 8, 64], F32)]
        cvf = [sb("cvf0", [128, 8, 64], F32), sb("cvf1", [128, 8, 64], F32)]
        ckT = [sb("ckT0", [128, 4, 128], BF16), sb("ckT1", [128, 4, 128], BF16)]
        cvb = [sb("cvb0", [128, 512], BF16), sb("cvb1", [128, 512], BF16)]
        ldi = 0
        for t in range(self.NTILE):
            sample = (t == NT)
            sl = t % 2
            r0 = t * 128
            T.dma("sp", xt[sl][:], S["X2"][r0:r0 + 128, :], reads=["DR_X2"], writes=[f"xt{sl}"])
            T.dma("sp", qt[sl][:], S["QT"][:, :, r0:r0 + 128].rearrange("c p t -> p c t"), reads=["DR_QT"], writes=[f"qt{sl}"])
            if not sample:
                T.op("pool", "tensor_copy", out=QZ[0:64, :, 0:128], in_=qt[sl][0:64, :, :], reads=[f"qt{sl}"], writes=["QZ"])
                T.op("pool", "tensor_copy", out=QZ[64:128, :, 128:256], in_=qt[sl][64:128, :, :], reads=[f"qt{sl}"], writes=["QZ"])
                for hg in range(4):
                    st["obank"] = self.reserve_banks(1)[0]
                    blocks = []
                    nb_total = t + 1
                    batches = []
                    jhi = t
                    while jhi >= 0:
                        jlo = max(0, jhi - KB + 1)
                        batches.append((jlo, jhi))
                        jhi = jlo - 1

                    def mk_load(jlo, jhi, bi, hg=hg):
                        def f():
                            nblk = jhi - jlo + 1
                            T.dma("sp", ktb[bi][:, :, 0:nblk * 128],
                                  S["KT"][2 * hg:2 * hg + 2, :, jlo * 128:(jhi + 1) * 128].rearrange("c p t -> p c t"),
                                  reads=["DR_KT"], writes=[f"ktb{bi}"])
                            T.dma("sp", vtb[bi][:, 0:nblk, :],
                                  S["VS"][jlo * 128:(jhi + 1) * 128, hg * 256:(hg + 1) * 256].rearrange("(n p) c -> p n c", p=128),
                                  reads=["DR_VS"], writes=[f"vtb{bi}"])
                        return f

                    bis = []
                    for (jlo, jhi) in batches:
                        bis.append(ldi % 3)
                        ldi += 1
                    mk_load(batches[0][0], batches[0][1], bis[0])()
                    for n_, (jlo, jhi) in enumerate(batches):
                        bi = bis[n_]
                        for j in range(jhi, jlo - 1, -1):
                            jj = j - jlo
                            zmms = [(ktb[bi][:, cl, jj * 128:(jj + 1) * 128], QZ[:, 2 * hg + cl, :], cl * 256, 256, [f"ktb{bi}", "QZ"])
                                    for cl in range(2)]
                            avmms = [(vtb[bi][:, jj, cl * 128:(cl + 1) * 128], cl * 256, 256, cl * 256, [f"vtb{bi}"]) for cl in range(2)]
                            pre = None
                            if j == jhi and n_ + 1 < len(batches):
                                pre = mk_load(batches[n_ + 1][0], batches[n_ + 1][1], bis[n_ + 1])
                            blocks.append({"nk": 128, "zmms": zmms, "avmms": avmms, "negm": negm_p[:] if j == t else None, "pre": pre})
                    self.sb_run_batched(blocks, st)
                    ob = st["obank"]
                    for cl in range(2):
                        c = 2 * hg + cl
                        T.op("act", "copy", out=oT[0:64, c, :], in_=self.ps[ob][0:64, cl * 256:cl * 256 + 128],
                             reads=[f"ps{ob}"], writes=["oT"])
                        T.op("act", "copy", out=oT[64:128, c, :], in_=self.ps[ob][64:128, cl * 256 + 128:cl * 256 + 256],
                             reads=[f"ps{ob}"], writes=["oT"])
                    self.release_banks()
            else:
                for s_ in range(2):
                    T.op("pool", "tensor_copy", out=QZs[0:64, :, s_, 0:64], in_=qt[sl][0:64, :, s_ * 64:(s_ + 1) * 64],
                         reads=[f"qt{sl}"], writes=["QZs"])
                    T.op("pool", "tensor_copy", out=QZs[64:128, :, s_, 64:128], in_=qt[sl][64:128, :, s_ * 64:(s_ + 1) * 64],
                         reads=[f"qt{sl}"], writes=["QZs"])
                npast = PAST // 128
                for s_ in range(2):
                    for hg2 in range(2):
                        st["obank"] = self.reserve_banks(1)[0]
                        st["i"] = 0
                        nblocks = npast + 1
                        pending = None
                        for bidx in range(nblocks):
                            bi = bidx % 2
                            if bidx == 0:
                                k0 = SEQ + s_ * 64
                                T.dma("sp", ktb[bi][:, 0, 0:256].rearrange("p (c k) -> p c k", c=4),
                                      S["KT"][4 * hg2:4 * hg2 + 4, :, k0:k0 + 64].rearrange("c p t -> p c t"),
                                      reads=["DR_KT"], writes=[f"ktb{bi}"])
                                T.dma("sp", vtb[bi][0:64, 0:2, :].rearrange("p a b -> p (a b)"),
                                      S["VS"][k0:k0 + 64, hg2 * 512:(hg2 + 1) * 512], reads=["DR_VS"], writes=[f"vtb{bi}"])
                                nk = 64
                                kts = [ktb[bi][:, 0, pl * 64:(pl + 1) * 64] for pl in range(4)]
                                vsrc = vtb[bi][0:64, 0:2, :].rearrange("p a b -> p (a b)")
                                kkeys, vkeys = [f"ktb{bi}"], [f"vtb{bi}"]
                                negm = negm_s[:]
                            else:
                                j = npast - bidx
                                T.dma("sp", ckf[bi][:], I["cache_k"][s_, 8 * hg2:8 * hg2 + 8, j * 128:(j + 1) * 128, :].rearrange("h k d -> k h d"),
                                      writes=[f"ckf{bi}"])
                                T.dma("sp", cvf[bi][:], I["cache_v"][s_, 8 * hg2:8 * hg2 + 8, j * 128:(j + 1) * 128, :].rearrange("h k d -> k h d"),
                                      writes=[f"cvf{bi}"])
                                self.transpose_to_bf16(ckf[bi][:].rearrange("p h d -> p (h d)"), f"ckf{bi}", ckT[bi], f"ckT{bi}", 4)
                                T.op("dve", "tensor_copy", out=cvb[bi][:], in_=cvf[bi][:].rearrange("p h d -> p (h d)"),
                                     reads=[f"cvf{bi}"], writes=[f"cvb{bi}"])
                                nk = 128
                                kts = [ckT[bi][:, pl, :] for pl in range(4)]
                                vsrc = cvb[bi][:]
                                kkeys, vkeys = [f"ckT{bi}"], [f"cvb{bi}"]
                                negm = None
                            zmms = [(kts[pl], QZs[:, 4 * hg2 + pl, s_, :], pl * 128, 128, kkeys + ["QZs"]) for pl in range(4)]
                            avmms = [(vsrc[:, pl * 128:(pl + 1) * 128], pl * 128, 128, pl * 128, vkeys) for pl in range(4)]
                            blk = {"nk": nk, "zmms": zmms, "avmms": avmms, "negm": negm}
                            self.sb_stage1(blk, st)
                            if pending is not None:
                                self.sb_stage2(pending, st, nblocks)
                            pending = blk
                        self.sb_stage2(pending, st, nblocks)
                        ob = st["obank"]
                        for pl in range(4):
                            c = 4 * hg2 + pl
                            T.op("act", "copy", out=oT[0:64, c, s_ * 64:(s_ + 1) * 64], in_=self.ps[ob][0:64, pl * 128:pl * 128 + 64],
                                 reads=[f"ps{ob}"], writes=["oT"])
                            T.op("act", "copy", out=oT[64:128, c, s_ * 64:(s_ + 1) * 64], in_=self.ps[ob][64:128, pl * 128 + 64:pl * 128 + 128],
                                 reads=[f"ps{ob}"], writes=["oT"])
                        self.release_banks()
            if "DBG_oT" in self.debug:
                T.dma("pool", S["DBG_oT"][t], oT[:], reads=["oT"], writes=["DR_dbg"])
            for j in range(2):
                b = self.nb()
                for kc in range(8):
                    T.op("pe", "matmul", self.ps[b][:], oT[:, kc, :], wo[:, kc, j * 512:(j + 1) * 512],
                         start=(kc == 0), stop=(kc == 7), reads=["wo", "oT"], writes=[f"ps{b}"])
                T.op("dve", "scalar_tensor_tensor", out=tin[:, j * 512:(j + 1) * 512], in0=xt[sl][:, j * 512:(j + 1) * 512],
                     scalar=ALPHA, in1=self.ps[b][:], op0=ALU.mult, op1=ALU.add, reads=[f"xt{sl}", f"ps{b}"], writes=["tin"])
            self.layer_norm_store(tin, "tin", gt, bt, xo[sl], f"xo{sl}", tmp)
            T.dma("pool", S["X3"][r0:r0 + 128, :], xo[sl][:], reads=[f"xo{sl}"], writes=["DR_X3"])

    def sb_run_batched(self, blocks, st):
        self.sb_run(blocks, st)


def _shard_inputs(inp, SEQ, PAST, consts):
    maps = []
    for c in range(NCORES):
        xin = np.concatenate([inp["x_prompt"][c], inp["x_sample"][2 * c:2 * c + 2].reshape(128, D)], axis=0)
        m = {
            "xin": np.ascontiguousarray(xin, dtype=np.float32),
            "state_in": np.ascontiguousarray(inp["state_ret"][0, 2 * c:2 * c + 2]),
            "ret_w_in": np.ascontiguousarray(inp["ret_w_in"][0]),
            "ret_w_o": np.ascontiguousarray(inp["ret_w_o"][0]),
            "ln_g": np.ascontiguousarray(inp["ln_g"].reshape(4, D)),
            "ln_b": np.ascontiguousarray(inp["ln_b"].reshape(4, D)),
        }
        for k in ("peer_w_q", "peer_keys_a", "peer_keys_b", "peer_u", "peer_v"):
            m[k] = np.ascontiguousarray(inp[k])
        m["sb_w_qkv"] = np.ascontiguousarray(inp["sb_w_qkv"][0])
        m["sb_w_o"] = np.ascontiguousarray(inp["sb_w_o"][0])
        m["cache_k"] = np.ascontiguousarray(inp["cache_k"][0, 2 * c:2 * c + 2])
        m["cache_v"] = np.ascontiguousarray(inp["cache_v"][0, 2 * c:2 * c + 2])
        for k, v in consts.items():
            if isinstance(v, np.ndarray):
                m[k] = v
        maps.append(m)
    return maps


def run(inp, SEQ, PAST, debug=(), upto=9):
    import time
    t0 = time.time()
    bld = Builder(SEQ, PAST, debug, upto)
    nc = bld.build()
    print("build time", time.time() - t0, "nins", bld.T.nins, "nwait", bld.T.nwait, flush=True)
    maps = _shard_inputs(inp, SEQ, PAST, bld.C)
    res = run_bass_kernel_spmd(nc, maps, core_ids=list(range(NCORES)))
    return bld, res.results


def kernel(**inp):
    inp = {k: np.asarray(v) for k, v in inp.items()}
    SEQ = inp["x_prompt"].shape[1]
    PAST = inp["cache_k"].shape[3]
    bld, r = run(inp, SEQ, PAST)
    B = inp["x_prompt"].shape[0]
    y_p = np.stack([r[c]["y"][:SEQ] for c in range(NCORES)])
    y_s = np.concatenate([r[c]["y"][SEQ:].reshape(2, DEC_SEQ, D) for c in range(NCORES)])
    ret_p = np.stack([r[c]["ret_p"] for c in range(NCORES)])[None]
    ret_s = np.concatenate([r[c]["ret_s"] for c in range(NCORES)])[None]
    kp = np.stack([r[c]["kp"] for c in range(NCORES)])[None]
    vp = np.stack([r[c]["vp"] for c in range(NCORES)])[None]
    ks = np.concatenate([r[c]["ks"] for c in range(NCORES)])[None]
    vs = np.concatenate([r[c]["vs"] for c in range(NCORES)])[None]
    return tuple(np.ascontiguousarray(a, dtype=np.float32) for a in (y_p, y_s, ret_p, ret_s, kp, vp, ks, vs))
```

```python
import math
from contextlib import ExitStack

import numpy as np
import concourse.bass as bass
import concourse.mybir as mybir
from concourse.bass_utils import run_bass_kernel_spmd

F32 = mybir.dt.float32
BF16 = mybir.dt.bfloat16
U32 = mybir.dt.uint32
I32 = mybir.dt.int32
AF = mybir.ActivationFunctionType
ALU = mybir.AluOpType
AX = mybir.AxisListType

D = 1024
NCORES = 8
DEC_SEQ = 64
ALPHA = 4 ** 0.25
LN_EPS = 1e-5
GN_EPS = 1e-6
RH = 4
RDK = 256
RDV = 512
SBH = 16
SBD = 64
ROPE_BASE = 10000.0
NEG = -30000.0
DUMMY_REV = 2


class Trk:
    ENGS = ("pe", "act", "dve", "pool", "sp")

    NDMA = 48

    def __init__(self, nc, es):
        dma_sems = tuple(f"dq{i}" for i in range(self.NDMA))
        self.dmap = {}
        self.nc = nc
        self.sems = {}
        self.cnt = {}
        self.seen = {e: {} for e in self.ENGS}
        self.lastw = {}
        self.readers = {}
        self.prog = {e: [] for e in self.ENGS}
        self.nwait = 0
        self.nins = 0
        self.vcs = {}
        for name in ("pe", "act", "dve", "pool") + tuple(dma_sems):
            self.sems[name] = es.enter_context(nc.semaphore(name))
            self.cnt[name] = 0

    def op(self, eng, meth, *args, reads=(), writes=(), sem=None, inc=1, **kw):
        reads = list(reads)
        writes = list(writes)
        for k in list(reads):
            if k.startswith("ps"):
                writes.append(k)
        deps = set()
        for k in reads:
            lw = self.lastw.get(k)
            if lw is not None:
                deps.add(lw)
        for k in writes:
            lw = self.lastw.get(k)
            if lw is not None:
                deps.add(lw)
            for s, v in self.readers.get(k, {}).items():
                deps.add((s, v))
        evc = self.seen[eng]
        waits = []
        for s, v in sorted(deps, key=lambda sv: -sv[1]):
            if eng == "pe" and s == "pe":
                continue
            if evc.get(s, 0) >= v:
                continue
            waits.append((s, v))
            self.nwait += 1
            vc = self.vcs.get((s, v))
            if vc is not None:
                for k2, v2 in vc:
                    if evc.get(k2, 0) < v2:
                        evc[k2] = v2
            if evc.get(s, 0) < v:
                evc[s] = v
        self.nins += 1
        s = sem or eng
        self.cnt[s] += inc
        v = self.cnt[s]
        self.prog[eng].append((waits, meth, args, kw, s, inc))
        snap = dict(evc)
        snap[s] = v
        self.vcs[(s, v)] = tuple(snap.items())
        for k in reads:
            r = self.readers.setdefault(k, {})
            r[s] = max(r.get(s, 0), v)
        for k in writes:
            self.lastw[k] = (s, v)
            self.readers[k] = {}

    def dma(self, q, out, in_, reads=(), writes=(), sem=None, sk=None, **kw):
        if sk is None:
            sk = writes[0] if (writes and not writes[0].startswith("DR_")) else reads[0]
        if sk not in self.dmap:
            assert len(self.dmap) < self.NDMA, "out of dma semaphores"
            self.dmap[sk] = f"dq{len(self.dmap)}"
        self.op(q, "dma_start", reads=reads, writes=writes, sem=self.dmap[sk], inc=16, out=out, in_=in_, **kw)

    def barrier(self):
        for e in self.ENGS:
            waits = []
            for s, v in self.cnt.items():
                if e == "pe" and s == "pe":
                    continue
                if v > 0 and self.seen[e].get(s, 0) < v:
                    waits.append((s, v))
                    self.seen[e][s] = v
            if waits:
                self.prog[e].append((waits, None, None, None, None, 0))
        self.vcs = {}

    def flush(self, es):
        self.barrier()
        self.dmap = {}
        block = es.enter_context(self.nc.Block())
        prog = self.prog
        self.prog = {e: [] for e in self.ENGS}
        if not hasattr(self, "real_base"):
            self.real_base = {s: 0 for s in self.sems}
            self.virt_base = {s: 0 for s in self.sems}
        comp = ("pe", "act", "dve", "pool")
        targets = {s: set() for s in comp}
        for e in self.ENGS:
            for waits, meth, args, kw, s, inc in prog[e]:
                for ws, wv in waits:
                    if ws in targets:
                        targets[ws].add(wv)
        rank = {}
        for s in comp:
            tl = sorted(targets[s])
            rank[s] = {v: self.real_base[s] + i + 1 for i, v in enumerate(tl)}
            assert all(v > self.virt_base[s] for v in tl), "wait on a pre-barrier value"
        virt = dict(self.virt_base)

        def replay(engname):
            def f(e):
                for waits, meth, args, kw, s, inc in prog[engname]:
                    ww = [(ws, rank[ws][wv] if ws in rank else wv) for ws, wv in waits]
                    if meth is None:
                        for ws, wv in ww:
                            e.wait_ge(self.sems[ws], wv)
                        continue
                    for ws, wv in ww[1:]:
                        e.wait_ge(self.sems[ws], wv)
                    ins = getattr(e, meth)(*args, **kw)
                    if ww:
                        ins._wait_ge(self.sems[ww[0][0]], ww[0][1])
                    if s in rank:
                        virt[s] += inc
                        if virt[s] in rank[s]:
                            ins.then_inc(self.sems[s], 1)
                    else:
                        ins.then_inc(self.sems[s], inc)
            return f

        block.sync(replay("sp"))
        block.tensor(replay("pe"))
        block.scalar(replay("act"))
        block.vector(replay("dve"))
        block.gpsimd(replay("pool"))
        for s in comp:
            self.real_base[s] += len(targets[s])
            self.virt_base[s] = self.cnt[s]


def _consts(SEQ, PAST):
    posmax = max(SEQ, PAST + DEC_SEQ)
    half = RDK // 2
    inv = (ROPE_BASE ** (-np.arange(half, dtype=np.float32) / half)).astype(np.float32)
    ang = inv[:, None] * np.arange(posmax, dtype=np.float32)[None, :]
    cosT = np.cos(ang).astype(np.float32)
    sinT = np.sin(ang).astype(np.float32)
    log_g = np.log1p(-np.exp2(-5.0 - np.arange(RH, dtype=np.float32))).astype(np.float32)
    kk = np.arange(128)
    tri = np.where(kk[:, None] >= kk[None, :], -1.0, 0.0).astype(np.float32)
    negm_p = np.tile(np.where(kk[:, None] >= kk[None, :], NEG, 0.0).astype(np.float32), (1, 4))
    k64 = np.arange(64)
    negm_s = np.tile(np.where(k64[:, None] >= k64[None, :], NEG, 0.0).astype(np.float32), (1, 8))
    out = {"cosT": cosT, "sinT": sinT, "tri": tri, "negm_p": negm_p, "negm_s": negm_s,
           "iota128": np.tile(np.arange(128, dtype=np.float32)[None, :], (128, 1))}
    for name, L, nstream in (("p", 128, 1), ("s", 64, 2)):
        pos = np.arange(L, dtype=np.float32)
        dm = np.zeros((128, RH, 128), np.float32)
        qd = np.zeros((128, RH, 128), np.float32)
        kd = np.zeros((128, RH), np.float32)
        for s in range(nstream):
            for h in range(RH):
                diff = pos[None, :] - pos[:, None]
                blk = np.where(diff >= 0, np.exp(log_g[h] * np.maximum(diff, 0.0)), 0.0)
                dm[s * L:(s + 1) * L, h, s * L:(s + 1) * L] = blk
                qd[:, h, s * L:(s + 1) * L] = np.exp(log_g[h] * (pos + 1.0))[None, :]
                kd[s * L:(s + 1) * L, h] = np.exp(log_g[h] * (L - 1.0 - pos))
        out["dm_" + name] = dm.astype(np.float32)
        out["qd_" + name] = qd.astype(np.float32)
        out["kd_" + name] = kd.astype(np.float32)
        out["sdec_" + name] = [float(np.exp(np.float32(log_g[h] * L))) for h in range(RH)]
    return out


class Builder:
    def __init__(self, SEQ, PAST, debug=(), upto=9):
        self.upto = upto
        self.SEQ, self.PAST = SEQ, PAST
        self.NT = SEQ // 128
        self.NTILE = self.NT + 1
        self.NTOK = SEQ + 128
        self.debug = set(debug)
        self.C = _consts(SEQ, PAST)
        self.nc = bass.Bass("TRN2", target_bir_lowering=False)
        self.bank_i = 0
        self.free_banks = list(range(8))

    def din(self, name, shape, dt=F32):
        return self.nc.dram_tensor(name, list(shape), dt, kind="ExternalInput").ap()

    def dout(self, name, shape, dt=F32):
        return self.nc.dram_tensor(name, list(shape), dt, kind="ExternalOutput").ap()

    def dscr(self, name, shape, dt):
        kind = "ExternalOutput" if name in self.debug else "Internal"
        return self.nc.dram_tensor(name, list(shape), dt, kind=kind).ap()

    def nb(self):
        b = self.free_banks[self.bank_i % len(self.free_banks)]
        self.bank_i += 1
        return b

    def reserve_banks(self, n):
        r = self.free_banks[-n:]
        self.free_banks = self.free_banks[:-n]
        return r

    def release_banks(self):
        self.free_banks = list(range(8))

    def build(self):
        nc = self.nc
        NTOK, SEQ, PAST = self.NTOK, self.SEQ, self.PAST
        posmax = self.C["cosT"].shape[1]
        I = self.I = {}
        I["xin"] = self.din("xin", [NTOK, D])
        I["state_in"] = self.din("state_in", [2, RH, RDK, RDV])
        I["ret_w_in"] = self.din("ret_w_in", [D, 6144])
        I["ret_w_o"] = self.din("ret_w_o", [2048, D])
        I["ln_g"] = self.din("ln_g", [4, D])
        I["ln_b"] = self.din("ln_b", [4, D])
        I["cosT"] = self.din("cosT", [128, posmax])
        I["sinT"] = self.din("sinT", [128, posmax])
        for n in ("p", "s"):
            I["dm_" + n] = self.din("dm_" + n, [128, RH, 128])
            I["qd_" + n] = self.din("qd_" + n, [128, RH, 128])
            I["kd_" + n] = self.din("kd_" + n, [128, RH])
        I["peer_w_q"] = self.din("peer_w_q", [2, D, 2048])
        I["peer_keys_a"] = self.din("peer_keys_a", [2, 8, 128, 128])
        I["peer_keys_b"] = self.din("peer_keys_b", [2, 8, 128, 128])
        I["peer_u"] = self.din("peer_u", [2, 16384, D])
        I["peer_v"] = self.din("peer_v", [2, 16384, D])
        I["iota128"] = self.din("iota128", [128, 128])
        I["sb_w_qkv"] = self.din("sb_w_qkv", [D, 3072])
        I["sb_w_o"] = self.din("sb_w_o", [D, D])
        I["cache_k"] = self.din("cache_k", [2, SBH, PAST, SBD])
        I["cache_v"] = self.din("cache_v", [2, SBH, PAST, SBD])
        I["tri"] = self.din("tri", [128, 128])
        I["negm_p"] = self.din("negm_p", [128, 512])
        I["negm_s"] = self.din("negm_s", [64, 512])
        O = self.O = {}
        O["ret_p"] = self.dout("ret_p", [RH, RDK, RDV])
        O["ret_s"] = self.dout("ret_s", [2, RH, RDK, RDV])
        O["kp"] = self.dout("kp", [SBH, SEQ, SBD])
        O["vp"] = self.dout("vp", [SBH, SEQ, SBD])
        O["ks"] = self.dout("ks", [2, SBH, DEC_SEQ, SBD])
        O["vs"] = self.dout("vs", [2, SBH, DEC_SEQ, SBD])
        O["y"] = self.dout("y", [NTOK, D])
        S = self.S = {}
        S["Yscr"] = self.dscr("Yscr", [NTOK, 2048], BF16)
        S["X1"] = self.dscr("X1", [NTOK, D], F32)
        S["X2"] = self.dscr("X2", [NTOK, D], F32)
        S["X3"] = self.dscr("X3", [NTOK, D], F32)
        if "DBG_oT" in self.debug:
            S["DBG_oT"] = self.dscr("DBG_oT", [self.NTILE, 128, 8, 128], BF16)
            S["DBG_W"] = self.dscr("DBG_W", [4, 128, 512], F32)
        S["VS"] = self.dscr("VS", [NTOK, D], BF16)
        S["QT"] = self.dscr("QT", [8, 128, NTOK], BF16)
        S["KT"] = self.dscr("KT", [8, 128, NTOK], BF16)
        S["UT"] = self.dscr("UT", [128, 128, 8, 128], BF16)
        S["VB"] = self.dscr("VB", [128, 128, D], BF16)

        with ExitStack() as es0:
            self.T = Trk(nc, es0)
            T = self.T
            self.ps = [es0.enter_context(nc.psum_tensor(f"ps{i}", [128, 512], F32)) for i in range(8)]
            self.ident_f = es0.enter_context(nc.sbuf_tensor("ident_f", [128, 128], F32))
            self.ident_b = es0.enter_context(nc.sbuf_tensor("ident_b", [128, 128], BF16))
            T.op("pool", "memset", self.ident_f[:], 0.0, writes=["ident_f"])
            T.op("pool", "memset", self.ident_b[:], float(DUMMY_REV), writes=["ident_b"])
            T.op("pool", "affine_select", out=self.ident_f[:], in_=self.ident_f[:], pattern=[[-1, 128]],
                 compare_op=ALU.not_equal, fill=1.0, base=0, channel_multiplier=1,
                 reads=["ident_f"], writes=["ident_f"])
            T.op("pool", "tensor_copy", out=self.ident_b[:], in_=self.ident_f[:], reads=["ident_f"], writes=["ident_b"])
            with ExitStack() as es:
                self.phase_retA(es)
                T.flush(es)
            with ExitStack() as es:
                self.phase_retB(es)
                T.flush(es)
            if self.upto >= 2:
                with ExitStack() as es:
                    self.phase_peer_prep(es, 0)
                    T.flush(es)
                with ExitStack() as es:
                    self.phase_peer(es, 0, S["X1"], "DR_X1", S["X2"], "DR_X2", 1)
                    T.flush(es)
            if self.upto >= 3:
                with ExitStack() as es:
                    self.phase_qkv(es)
                    T.flush(es)
            if self.upto >= 4:
                with ExitStack() as es:
                    self.phase_att(es)
                    T.flush(es)
            if self.upto >= 5:
                with ExitStack() as es:
                    self.phase_peer_prep(es, 1)
                    T.flush(es)
                with ExitStack() as es:
                    self.phase_peer(es, 1, S["X3"], "DR_X3", O["y"], "DR_y", 3)
                    T.flush(es)
            with ExitStack() as es:
                T.flush(es)
        return nc

    def load_weight_bf16(self, es_unused, dst, dst_key, src2d, ncols, stage, kchunks, col0=0):
        T, nc = self.T, self.nc
        step = stage["cols"]
        i = 0
        for c0 in range(0, ncols, step):
            cw = min(step, ncols - c0)
            slot = stage["i"] % 2
            stage["i"] += 1
            st = stage["t"][slot]
            key = stage["keys"][slot] if "keys" in stage else f"wstage{slot}"
            T.dma("sp", st[:, 0:kchunks, 0:cw],
                  src2d[:, col0 + c0:col0 + c0 + cw].rearrange("(kc p) n -> p kc n", p=128),
                  writes=[key])
            eng = ("dve", "act")[i % 2]
            i += 1
            if eng == "dve":
                T.op("dve", "tensor_copy", out=dst[:, :, c0:c0 + cw], in_=st[:, 0:kchunks, 0:cw], reads=[key], writes=[dst_key])
            else:
                T.op("act", "copy", out=dst[:, :, c0:c0 + cw], in_=st[:, 0:kchunks, 0:cw], reads=[key], writes=[dst_key])

    def transpose_to_bf16(self, src_f32, src_key, dst, dst_key, nchunks):
        T = self.T
        for g0 in range(0, nchunks, 4):
            b = self.nb()
            pk = f"ps{b}"
            n = min(4, nchunks - g0)
            for j in range(n):
                T.op("pe", "transpose", self.ps[b][:, j * 128:(j + 1) * 128],
                     src_f32[:, (g0 + j) * 128:(g0 + j + 1) * 128], self.ident_f[:],
                     reads=[src_key, "ident_f"], writes=[pk])
            T.op("act", "copy", out=dst[:, g0:g0 + n, :],
                 in_=self.ps[b][:, 0:n * 128].rearrange("p (c n) -> p c n", c=n),
                 reads=[pk], writes=[dst_key])

    def layer_norm_store(self, tin, tin_key, gt, bt, out_t, out_key, tmp):
        T = self.T
        bst, mv, rs = tmp["bst"], tmp["mv"], tmp["rs"]
        for j in range(2):
            T.op("dve", "bn_stats", out=bst[:, j, :], in_=tin[:, j * 512:(j + 1) * 512], reads=[tin_key], writes=["ln_bst"])
        T.op("dve", "bn_aggr", out=mv[:], in_=bst[:].rearrange("p a b -> p (a b)"), reads=["ln_bst"], writes=["ln_mv"])
        T.op("act", "activation", out=rs[:, 0:1], in_=mv[:, 1:2], func=AF.Sqrt, bias=LN_EPS, scale=1.0,
             reads=["ln_mv"], writes=["ln_rs"])
        T.op("dve", "reciprocal", out=rs[:, 1:2], in_=rs[:, 0:1], reads=["ln_rs"], writes=["ln_rs"])
        T.op("dve", "tensor_scalar", out=out_t[:], in0=tin[:], scalar1=mv[:, 0:1], scalar2=rs[:, 1:2],
             op0=ALU.subtract, op1=ALU.mult, reads=[tin_key, "ln_mv", "ln_rs"], writes=[out_key])
        T.op("pool", "tensor_tensor", out=out_t[:], in0=out_t[:], in1=gt[:], op=ALU.mult, reads=[out_key, "lng"], writes=[out_key])
        T.op("pool", "tensor_tensor", out=out_t[:], in0=out_t[:], in1=bt[:], op=ALU.add, reads=[out_key, "lnb"], writes=[out_key])

    def phase_retA(self, es):
        nc, T, I, O, S = self.nc, self.T, self.I, self.O, self.S
        NT = self.NT
        sb = lambda name, shape, dt: es.enter_context(nc.sbuf_tensor("A_" + name, list(shape), dt))
        wqkv = sb("wqkv", [128, 8, 4096], BF16)
        stage = {"t": [sb("wst0", [128, 8, 256], F32), sb("wst1", [128, 8, 256], F32)], "i": 0, "cols": 256}
        self.load_weight_bf16(es, wqkv, "wqkv", I["ret_w_in"], 4096, stage, 8)
        Sf = [sb("Sf0", [128, RH, 2, RDV], F32), sb("Sf1", [128, RH, 2, RDV], F32)]
        Sb = [sb("Sb0", [128, RH, 2, RDV], BF16), sb("Sb1", [128, RH, 2, RDV], BF16)]
        xt = [sb("xt0", [128, D], F32), sb("xt1", [128, D], F32)]
        xT = [sb("xT0", [128, 8, 128], BF16), sb("xT1", [128, 8, 128], BF16)]
        cs = [sb("cs0", [128, 2, 128], F32), sb("cs1", [128, 2, 128], F32)]
        qT = sb("qT", [128, 8, 128], BF16)
        qdT = sb("qdT", [128, 8, 128], BF16)
        qdT2 = [sb("qdTa", [128, 8, 128], BF16), sb("qdTb", [128, 8, 128], BF16)]
        kT = sb("kT", [128, 8, 128], BF16)
        tA = sb("tA", [128, RH, 128], F32)
        tB = sb("tB", [128, RH, 128], F32)
        tC = sb("tC", [128, RH, 128], F32)
        tD = sb("tD", [128, RH, 128], F32)
        ktm = sb("ktm", [128, RH, 256], BF16)
        vv = sb("vv", [128, RH, RDV], BF16)
        vd = sb("vd", [128, RH, RDV], BF16)
        ATm = sb("ATm", [128, RH, 128], BF16)
        Yt = [sb("Y0", [128, 2048], BF16), sb("Y1", [128, 2048], BF16)]
        dm = {n: sb("dm_" + n, [128, RH, 128], F32) for n in "ps"}
        qd = {n: sb("qd_" + n, [128, RH, 128], F32) for n in "ps"}
        kd = {n: sb("kd_" + n, [128, RH], F32) for n in "ps"}
        bst = sb("gn_bst", [128, RH, 6], F32)
        mv = sb("gn_mv", [128, RH, 2], F32)
        rs = sb("gn_rs", [128, RH, 3], F32)
        for n in "ps":
            T.dma("sp", dm[n][:], I["dm_" + n], writes=["dm_" + n])
            T.dma("sp", qd[n][:], I["qd_" + n], writes=["qd_" + n])
            T.dma("sp", kd[n][:], I["kd_" + n], writes=["kd_" + n])
        T.op("pool", "memset", Sf[0][:], 0.0, writes=["Sf0"])
        T.op("pool", "memset", Sb[0][:], 0.0, writes=["Sb0"])
        T.op("pool", "memset", qdT2[0][:], 0.0, writes=["qdTa"])
        T.op("pool", "memset", qdT2[1][:], 0.0, writes=["qdTb"])

        for t in range(self.NTILE):
            sample = (t == NT)
            n = "s" if sample else "p"
            sl = t % 2
            r0 = t * 128
            xk, xTk, csk, Yk = f"xt{sl}", f"xT{sl}", f"cs{sl}", f"Y{sl}"
            if sample:
                T.dma("pool", O["ret_p"].rearrange("h (dc p) e -> p h dc e", p=128), Sf[0][:], reads=["Sf0"])
                for s in range(2):
                    T.dma("sp", Sf[s][:], I["state_in"][s].rearrange("h (dc p) e -> p h dc e", p=128), writes=[f"Sf{s}"])
                    T.op("pool", "tensor_copy", out=Sb[s][:], in_=Sf[s][:], reads=[f"Sf{s}"], writes=[f"Sb{s}"])
            T.dma("sp", xt[sl][:], I["xin"][r0:r0 + 128, :], writes=[xk])
            if sample:
                for s in range(2):
                    T.dma("sp", cs[sl][:, 0, s * 64:(s + 1) * 64], I["cosT"][:, self.PAST:self.PAST + 64], writes=[csk])
                    T.dma("sp", cs[sl][:, 1, s * 64:(s + 1) * 64], I["sinT"][:, self.PAST:self.PAST + 64], writes=[csk])
            else:
                T.dma("sp", cs[sl][:, 0, :], I["cosT"][:, r0:r0 + 128], writes=[csk])
                T.dma("sp", cs[sl][:, 1, :], I["sinT"][:, r0:r0 + 128], writes=[csk])
            self.transpose_to_bf16(xt[sl], xk, xT[sl], xTk, 8)
            cosb = cs[sl][:, 0:1, :].to_broadcast([128, RH, 128])
            sinb = cs[sl][:, 1:2, :].to_broadcast([128, RH, 128])
            for which, dstT, col_base, scl in (("q", qT, 0, 1.0), ("k", kT, 1024, RDK ** -0.5)):
                banks = []
                for half in range(2):
                    b = self.nb()
                    banks.append(b)
                    for h in range(RH):
                        c0 = col_base + h * 256 + half * 128
                        for kc in range(8):
                            T.op("pe", "matmul", self.ps[b][:, h * 128:(h + 1) * 128], wqkv[:, kc, c0:c0 + 128],
                                 xT[sl][:, kc, :], start=(kc == 0), stop=(kc == 7),
                                 reads=["wqkv", xTk], writes=[f"ps{b}"])
                p1 = self.ps[banks[0]][:].rearrange("p (h n) -> p h n", h=RH)
                p2 = self.ps[banks[1]][:].rearrange("p (h n) -> p h n", h=RH)
                k1, k2 = f"ps{banks[0]}", f"ps{banks[1]}"
                T.op("dve", "scalar_tensor_tensor", out=tA[:], in0=p1, scalar=scl, in1=cosb, op0=ALU.mult, op1=ALU.mult,
                     reads=[k1, csk], writes=["tA"])
                T.op("dve", "scalar_tensor_tensor", out=tC[:], in0=p1, scalar=scl, in1=sinb, op0=ALU.mult, op1=ALU.mult,
                     reads=[k1, csk], writes=["tC"])
                T.op("dve", "scalar_tensor_tensor", out=tB[:], in0=p2, scalar=scl, in1=sinb, op0=ALU.mult, op1=ALU.mult,
                     reads=[k2, csk], writes=["tB"])
                T.op("dve", "scalar_tensor_tensor", out=tD[:], in0=p2, scalar=scl, in1=cosb, op0=ALU.mult, op1=ALU.mult,
                     reads=[k2, csk], writes=["tD"])
                T.op("pool", "tensor_tensor", out=dstT[:, 0:4, :], in0=tA[:], in1=tB[:], op=ALU.subtract,
                     reads=["tA", "tB"], writes=[which + "T"])
                T.op("pool", "tensor_tensor", out=dstT[:, 4:8, :], in0=tC[:], in1=tD[:], op=ALU.add,
                     reads=["tC", "tD"], writes=[which + "T"])
            for half in range(2):
                T.op("pool", "tensor_tensor", out=qdT[:, half * 4:(half + 1) * 4, :], in0=qT[:, half * 4:(half + 1) * 4, :],
                     in1=qd[n][:], op=ALU.mult, reads=["qT", "qd_" + n], writes=["qdT"])
            if sample:
                for s in range(2):
                    T.op("pool", "tensor_copy", out=qdT2[s][:, :, s * 64:(s + 1) * 64], in_=qdT[:, :, s * 64:(s + 1) * 64],
                         reads=["qdT"], writes=["qdT" + "ab"[s]])
            for h in range(RH):
                b = self.nb()
                for kc in range(8):
                    T.op("pe", "matmul", self.ps[b][:], xT[sl][:, kc, :], wqkv[:, kc, 2048 + h * 512:2048 + (h + 1) * 512],
                         start=(kc == 0), stop=(kc == 7), reads=["wqkv", xTk], writes=[f"ps{b}"])
                T.op("act", "copy", out=vv[:, h, :], in_=self.ps[b][:], reads=[f"ps{b}"], writes=["vv"])
                T.op("act", "mul", out=vd[:, h, :], in_=self.ps[b][:], mul=kd[n][:, h:h + 1], reads=[f"ps{b}", "kd_" + n], writes=["vd"])
            b = self.nb()
            pkb = self.ps[b][:].bitcast(BF16)
            for h in range(RH):
                for half in range(2):
                    cc = h * 2 + half
                    T.op("pe", "transpose", pkb[:, cc * 128:(cc + 1) * 128], kT[:, half * 4 + h, :], self.ident_b[:],
                         reads=["kT", "ident_b"], writes=[f"ps{b}"])
            T.op("act", "copy", out=ktm[:].rearrange("p h d -> p (h d)"), in_=pkb[:, 0:1024], reads=[f"ps{b}"], writes=["ktm"])
            b = self.nb()
            for h in range(RH):
                for half in range(2):
                    T.op("pe", "matmul", self.ps[b][:, h * 128:(h + 1) * 128], kT[:, half * 4 + h, :], qT[:, half * 4 + h, :],
                         start=(half == 0), stop=(half == 1), reads=["kT", "qT"], writes=[f"ps{b}"])
            T.op("dve", "tensor_tensor", out=ATm[:], in0=self.ps[b][:].rearrange("p (h n) -> p h n", h=RH), in1=dm[n][:],
                 op=ALU.mult, reads=[f"ps{b}", "dm_" + n], writes=["ATm"])
            obanks = []
            for h in range(RH):
                b = self.nb()
                obanks.append(b)
                T.op("pe", "matmul", self.ps[b][:], ATm[:, h, :], vv[:, h, :], start=True, stop=False,
                     reads=["ATm", "vv"], writes=[f"ps{b}"])
                srcs = [(qdT2[s], "qdT" + "ab"[s], s) for s in range(2)] if sample else [(qdT, "qdT", 0)]
                nmm = len(srcs) * 2
                i = 0
                for (qsrc, qkey, s) in srcs:
                    for half in range(2):
                        i += 1
                        T.op("pe", "matmul", self.ps[b][:], qsrc[:, half * 4 + h, :], Sb[s][:, h, half, :],
                             start=False, stop=(i == nmm), reads=[qkey, f"Sb{s}"], writes=[f"ps{b}"])
            for h in range(RH):
                b = obanks[h]
                T.op("dve", "bn_stats", out=bst[:, h, :], in_=self.ps[b][:], reads=[f"ps{b}"], writes=["gn_bst"])
            for h in range(RH):
                T.op("dve", "bn_aggr", out=mv[:, h, :], in_=bst[:, h, :], reads=["gn_bst"], writes=["gn_mv"])
            T.op("act", "activation", out=rs[:, :, 0], in_=mv[:, :, 1], func=AF.Sqrt, bias=GN_EPS, scale=1.0,
                 reads=["gn_mv"], writes=["gn_rs"])
            T.op("dve", "reciprocal", out=rs[:, :, 1], in_=rs[:, :, 0], reads=["gn_rs"], writes=["gn_rs"])
            T.op("dve", "scalar_tensor_tensor", out=rs[:, :, 2], in0=mv[:, :, 0], scalar=-1.0, in1=rs[:, :, 1],
                 op0=ALU.mult, op1=ALU.mult, reads=["gn_mv", "gn_rs"], writes=["gn_rs"])
            for h in range(RH):
                b = obanks[h]
                T.op("act", "activation", out=Yt[sl][:, h * 512:(h + 1) * 512], in_=self.ps[b][:], func=AF.Identity,
                     scale=rs[:, h, 1:2], bias=rs[:, h, 2:3], reads=[f"ps{b}", "gn_rs"], writes=[Yk])
            T.dma("pool", S["Yscr"][r0:r0 + 128, :], Yt[sl][:], reads=[Yk], writes=["DR_Yscr"])
            sdec = self.C["sdec_" + n]
            for h in range(RH):
                for dc in range(2):
                    for s in range(2 if sample else 1):
                        b = self.nb()
                        rows = slice(s * 64, (s + 1) * 64) if sample else slice(0, 128)
                        T.op("pe", "matmul", self.ps[b][:], ktm[rows, h, dc * 128:(dc + 1) * 128], vd[rows, h, :],
                             start=True, stop=True, reads=["ktm", "vd"], writes=[f"ps{b}"])
                        T.op("dve", "scalar_tensor_tensor", out=Sf[s][:, h, dc, :], in0=Sf[s][:, h, dc, :], scalar=sdec[h],
                             in1=self.ps[b][:], op0=ALU.mult, op1=ALU.add, reads=[f"Sf{s}", f"ps{b}"], writes=[f"Sf{s}"])
                        if not sample:
                            T.op("pool", "tensor_copy", out=Sb[s][:, h, dc, :], in_=Sf[s][:, h, dc, :],
                                 reads=[f"Sf{s}"], writes=[f"Sb{s}"])
        for s in range(2):
            T.dma("pool", O["ret_s"][s].rearrange("h (dc p) e -> p h dc e", p=128), Sf[s][:], reads=[f"Sf{s}"])

    def phase_retB(self, es):
        nc, T, I, O, S = self.nc, self.T, self.I, self.O, self.S
        sb = lambda name, shape, dt: es.enter_context(nc.sbuf_tensor("B_" + name, list(shape), dt))
        wg = sb("wg", [128, 8, 2048], BF16)
        wo = sb("wo", [128, 16, 1024], BF16)
        stage = {"t": [sb("wst0", [128, 16, 256], F32), sb("wst1", [128, 16, 256], F32)], "i": 0, "cols": 256}
        self.load_weight_bf16(es, wg, "wg", I["ret_w_in"], 2048, stage, 8, col0=4096)
        self.load_weight_bf16(es, wo, "wo", I["ret_w_o"], 1024, stage, 16)
        gt = sb("lng", [128, D], F32)
        bt = sb("lnb", [128, D], F32)
        T.dma("sp", gt[:], I["ln_g"][0:1, :].partition_broadcast(128)[:, 0, :], writes=["lng"])
        T.dma("sp", bt[:], I["ln_b"][0:1, :].partition_broadcast(128)[:, 0, :], writes=["lnb"])
        xt = [sb("xt0", [128, D], F32), sb("xt1", [128, D], F32)]
        xT = [sb("xT0", [128, 8, 128], BF16), sb("xT1", [128, 8, 128], BF16)]
        Yt = [sb("Y0", [128, 2048], BF16), sb("Y1", [128, 2048], BF16)]
        sg = sb("sg", [128, 2048], BF16)
        gy = sb("gy", [128, 2048], BF16)
        gyT = sb("gyT", [128, 16, 128], BF16)
        tin = sb("tin", [128, D], F32)
        xo = [sb("xo0", [128, D], F32), sb("xo1", [128, D], F32)]
        tmp = {"bst": sb("ln_bst", [128, 2, 6], F32), "mv": sb("ln_mv", [128, 2], F32), "rs": sb("ln_rs", [128, 2], F32)}
        for t in range(self.NTILE):
            sl = t % 2
            r0 = t * 128
            xk, xTk, Yk, xok = f"xt{sl}", f"xT{sl}", f"Y{sl}", f"xo{sl}"
            T.dma("sp", xt[sl][:], I["xin"][r0:r0 + 128, :], writes=[xk])
            T.dma("sp", Yt[sl][:], S["Yscr"][r0:r0 + 128, :], reads=["DR_Yscr"], writes=[Yk])
            self.transpose_to_bf16(xt[sl], xk, xT[sl], xTk, 8)
            for j in range(4):
                b = self.nb()
                for kc in range(8):
                    T.op("pe", "matmul", self.ps[b][:], xT[sl][:, kc, :], wg[:, kc, j * 512:(j + 1) * 512],
                         start=(kc == 0), stop=(kc == 7), reads=["wg", xTk], writes=[f"ps{b}"])
                T.op("act", "activation", out=sg[:, j * 512:(j + 1) * 512], in_=self.ps[b][:], func=AF.Silu,
                     reads=[f"ps{b}"], writes=["sg"])
            T.op("pool", "tensor_tensor", out=gy[:], in0=sg[:], in1=Yt[sl][:], op=ALU.mult, reads=["sg", Yk], writes=["gy"])
            for g0 in range(0, 16, 8):
                b = self.nb()
                pkb = self.ps[b][:].bitcast(BF16)
                for j in range(8):
                    T.op("pe", "transpose", pkb[:, j * 128:(j + 1) * 128], gy[:, (g0 + j) * 128:(g0 + j + 1) * 128],
                         self.ident_b[:], reads=["gy", "ident_b"], writes=[f"ps{b}"])
                T.op("act", "copy", out=gyT[:, g0:g0 + 8, :], in_=pkb[:, 0:1024].rearrange("p (c n) -> p c n", c=8),
                     reads=[f"ps{b}"], writes=["gyT"])
            for j in range(2):
                b = self.nb()
                for kc in range(16):
                    T.op("pe", "matmul", self.ps[b][:], gyT[:, kc, :], wo[:, kc, j * 512:(j + 1) * 512],
                         start=(kc == 0), stop=(kc == 15), reads=["wo", "gyT"], writes=[f"ps{b}"])
                T.op("dve", "scalar_tensor_tensor", out=tin[:, j * 512:(j + 1) * 512], in0=xt[sl][:, j * 512:(j + 1) * 512],
                     scalar=ALPHA, in1=self.ps[b][:], op0=ALU.mult, op1=ALU.add, reads=[xk, f"ps{b}"], writes=["tin"])
            self.layer_norm_store(tin, "tin", gt, bt, xo[sl], xok, tmp)
            T.dma("pool", S["X1"][r0:r0 + 128, :], xo[sl][:], reads=[xok], writes=["DR_X1"])

    def phase_peer_prep(self, es, li):
        nc, T, I, S = self.nc, self.T, self.I, self.S
        sb = lambda name, shape, dt: es.enter_context(nc.sbuf_tensor(f"PP{li}_" + name, list(shape), dt))
        uf = [sb("uf0", [128, D], F32), sb("uf1", [128, D], F32)]
        vf = [sb("vf0", [128, D], F32), sb("vf1", [128, D], F32)]
        ub = [sb("ub0", [128, 8, 128], BF16), sb("ub1", [128, 8, 128], BF16)]
        vb = [sb("vb0", [128, D], BF16), sb("vb1", [128, D], BF16)]
        for a in range(128):
            sl = a % 2
            T.dma("sp", uf[sl][:], I["peer_u"][li, a * 128:(a + 1) * 128, :], writes=[f"uf{sl}"])
            T.dma("sp", vf[sl][:], I["peer_v"][li, a * 128:(a + 1) * 128, :], writes=[f"vf{sl}"])
            self.transpose_to_bf16(uf[sl], f"uf{sl}", ub[sl], f"ub{sl}", 8)
            T.op("dve", "tensor_copy", out=vb[sl][:], in_=vf[sl][:], reads=[f"vf{sl}"], writes=[f"vb{sl}"])
            T.dma("pool", S["UT"][a], ub[sl][:], reads=[f"ub{sl}"], writes=["DR_UT"])
            T.dma("pool", S["VB"][a], vb[sl][:], reads=[f"vb{sl}"], writes=["DR_VB"])

    def phase_peer(self, es, li, xin_ap, xin_key, xout_ap, xout_key, ln_i):
        nc, T, I, S = self.nc, self.T, self.I, self.S
        sb = lambda name, shape, dt: es.enter_context(nc.sbuf_tensor(f"PM{li}_" + name, list(shape), dt))
        big = [sb(f"big{i}", [128, 2048], F32) for i in range(3)]
        wq = sb("wq", [128, 8, 2048], BF16)
        stage = {"t": [big[0][:].rearrange("p (a b) -> p a b", a=8), big[1][:].rearrange("p (a b) -> p a b", a=8)],
                 "i": 0, "cols": 256, "keys": ["big0", "big1"]}
        self.load_weight_bf16(es, wq, "wq", I["peer_w_q"][li], 2048, stage, 8)
        keysT = sb("keysT", [128, 16, 128], BF16)
        kf = big[2][:].rearrange("p (c d) -> p c d", c=16)
        for h in range(8):
            T.dma("sp", kf[:, 2 * h, :], I["peer_keys_a"][li, h], writes=["big2"])
            T.dma("sp", kf[:, 2 * h + 1, :], I["peer_keys_b"][li, h], writes=["big2"])
        self.transpose_to_bf16(big[2], "big2", keysT, "keysT", 16)
        gt = sb("lng", [128, D], F32)
        bt = sb("lnb", [128, D], F32)
        T.dma("sp", gt[:], I["ln_g"][ln_i:ln_i + 1, :].partition_broadcast(128)[:, 0, :], writes=["lng"])
        T.dma("sp", bt[:], I["ln_b"][ln_i:ln_i + 1, :].partition_broadcast(128)[:, 0, :], writes=["lnb"])
        iota = sb("iota", [128, 128], F32)
        T.dma("sp", iota[:], I["iota128"], writes=["iota"])
        iota16x = sb("iota16x", [128, 16], F32)
        T.op("dve", "tensor_scalar", out=iota16x[:], in0=iota[:, 0:16], scalar1=16.0, scalar2=None, op0=ALU.mult,
             reads=["iota"], writes=["iota16x"])
        GT = sb("GT", [128, 128, 256], BF16)
        xTs = [sb("xTs0", [128, 8, 256], BF16), sb("xTs1", [128, 8, 256], BF16)]
        xt = [[sb(f"xt{p}{ti}", [128, D], F32) for ti in range(2)] for p in range(2)]
        xTt = sb("xTt", [128, 8, 128], BF16)
        qT = sb("qT", [128, 16, 128], BF16)
        v16 = sb("v16", [128, 16, 16], F32)
        i16 = sb("i16", [128, 16, 16], U32)
        i16f = sb("i16f", [128, 16, 16], F32)
        s16 = sb("s16", [128, 8, 16], F32)
        pos = sb("pos", [128, 8, 16], U32)
        posf = sb("posf", [128, 8, 16], F32)
        jbu = sb("jbu", [128, 8, 16], U32)
        jbf = sb("jbf", [128, 8, 16], F32)
        jaf = sb("jaf", [128, 8, 16], F32)
        sel = sb("sel", [128, 3, 128], F32)
        selT = [[sb(f"selT{p}{ti}", [128, 3, 128], F32) for ti in range(2)] for p in range(2)]
        zz = sb("zz", [128, 8, 2], F32)
        OA = sb("OA", [128, 16, 128], BF16)
        OB = sb("OB", [128, 16, 128], BF16)
        utb = [sb(f"ut{i}", [128, 8, 128], BF16) for i in range(3)]
        vtb = [sb(f"vt{i}", [128, D], BF16) for i in range(3)]
        hg = [sb(f"hg{i}", [128, 256], BF16) for i in range(2)]
        GH = [sb(f"GH{i}", [128, 256], BF16) for i in range(2)]
        tin = sb("tin", [128, D], F32)
        xo = [sb("xo0", [128, D], F32)]
        tmp = {"bst": sb("ln_bst", [128, 2, 6], F32), "mv": sb("ln_mv", [128, 2], F32), "rs": sb("ln_rs", [128, 2], F32)}
        sc, work, cand = big
        cwork = work
        NTILE = self.NTILE
        nsb = (NTILE + 1) // 2
        def routing(sbi):
            p = sbi % 2
            tiles = [t for t in (2 * sbi, 2 * sbi + 1) if t < NTILE]
            for ti, t in enumerate(tiles):
                r0 = t * 128
                xk = f"xt{p}{ti}"
                T.dma("sp", xt[p][ti][:], xin_ap[r0:r0 + 128, :], reads=[xin_key], writes=[xk])
                self.transpose_to_bf16(xt[p][ti], xk, xTt, "xTt", 8)
                T.op("pool", "tensor_copy", out=xTs[p][:, :, ti * 128:(ti + 1) * 128], in_=xTt[:], reads=["xTt"], writes=[f"xTs{p}"])
                for g in range(4):
                    b = self.nb()
                    for j in range(4):
                        c = g * 4 + j
                        for kc in range(8):
                            T.op("pe", "matmul", self.ps[b][:, j * 128:(j + 1) * 128], wq[:, kc, c * 128:(c + 1) * 128],
                                 xTt[:, kc, :], start=(kc == 0), stop=(kc == 7), reads=["wq", "xTt"], writes=[f"ps{b}"])
                    T.op("act", "copy", out=qT[:, g * 4:(g + 1) * 4, :], in_=self.ps[b][:].rearrange("p (c n) -> p c n", c=4),
                         reads=[f"ps{b}"], writes=["qT"])
                    yield
                for g in range(4):
                    b = self.nb()
                    for j in range(4):
                        c = g * 4 + j
                        T.op("pe", "matmul", self.ps[b][:, j * 128:(j + 1) * 128], qT[:, c, :], keysT[:, c, :],
                             start=True, stop=True, reads=["qT", "keysT"], writes=[f"ps{b}"])
                    T.op("act", "copy", out=sc[:, g * 512:(g + 1) * 512], in_=self.ps[b][:], reads=[f"ps{b}"], writes=["big0"])
                    yield
                for c in range(16):
                    scc = sc[:, c * 128:(c + 1) * 128]
                    wkc = work[:, c * 128:(c + 1) * 128]
                    T.op("dve", "max", out=v16[:, c, 0:8], in_=scc, reads=["big0"], writes=["v16"])
                    T.op("dve", "match_replace", out=wkc, in_to_replace=v16[:, c, 0:8], in_values=scc, imm_value=-1e30,
                         reads=["big0", "v16"], writes=["big1"])
                    T.op("dve", "max", out=v16[:, c, 8:16], in_=wkc, reads=["big1"], writes=["v16"])
                    T.op("dve", "max_index", out=i16[:, c, 0:8], in_max=v16[:, c, 0:8], in_values=scc,
                         reads=["big0", "v16"], writes=["i16"])
                    T.op("dve", "max_index", out=i16[:, c, 8:16], in_max=v16[:, c, 8:16], in_values=scc,
                         reads=["big0", "v16"], writes=["i16"])
                    yield
                T.op("dve", "tensor_copy", out=i16f[:], in_=i16[:], reads=["i16"], writes=["i16f"])
                v4 = v16[:].rearrange("p (h two) k -> p h two k", two=2)
                i4 = i16f[:].rearrange("p (h two) k -> p h two k", two=2)
                cand4 = cand[:].rearrange("p (h i j) -> p h i j", h=8, i=16)
                T.op("dve", "tensor_tensor", out=cand4, in0=v4[:, :, 0, :].unsqueeze(3).to_broadcast([128, 8, 16, 16]),
                     in1=v4[:, :, 1, :].unsqueeze(2).to_broadcast([128, 8, 16, 16]), op=ALU.add,
                     reads=["v16"], writes=["big2"])
                for h in range(8):
                    cc = cand[:, h * 256:(h + 1) * 256]
                    cw = cwork[:, h * 256:(h + 1) * 256]
                    T.op("dve", "max", out=s16[:, h, 0:8], in_=cc, reads=["big2"], writes=["s16"])
                    T.op("dve", "match_replace", out=cw, in_to_replace=s16[:, h, 0:8], in_values=cc, imm_value=-1e30,
                         reads=["big2", "s16"], writes=["big1"])
                    T.op("dve", "max", out=s16[:, h, 8:16], in_=cw, reads=["big1"], writes=["s16"])
                    T.op("dve", "max_index", out=pos[:, h, 0:8], in_max=s16[:, h, 0:8], in_values=cc,
                         reads=["big2", "s16"], writes=["pos"])
                    T.op("dve", "max_index", out=pos[:, h, 8:16], in_max=s16[:, h, 8:16], in_values=cc,
                         reads=["big2", "s16"], writes=["pos"])
                    yield
                T.op("dve", "tensor_copy", out=posf[:], in_=pos[:], reads=["pos"], writes=["posf"])
                T.op("dve", "tensor_single_scalar", out=jbu[:], in_=pos[:], scalar=15, op=ALU.bitwise_and, reads=["pos"], writes=["jbu"])
                T.op("dve", "tensor_copy", out=jbf[:], in_=jbu[:], reads=["jbu"], writes=["jbf"])
                T.op("dve", "tensor_tensor", out=jaf[:], in0=posf[:], in1=jbf[:], op=ALU.subtract, reads=["posf", "jbf"], writes=["jaf"])
                E4 = sc[:].rearrange("p (h k i) -> p h k i", h=8, k=16)
                for w_, (jsrc, jkey, iot) in enumerate(((jaf, "jaf", iota16x), (jbf, "jbf", iota))):
                    T.op("dve", "tensor_tensor", out=E4,
                         in0=iot[:, 0:16].unsqueeze(1).unsqueeze(1).to_broadcast([128, 8, 16, 16]),
                         in1=jsrc[:].unsqueeze(3).to_broadcast([128, 8, 16, 16]), op=ALU.is_equal,
                         reads=["iota", "iota16x", jkey], writes=["big0"])
                    T.op("dve", "tensor_tensor", out=E4, in0=E4,
                         in1=i4[:, :, w_, :].unsqueeze(2).to_broadcast([128, 8, 16, 16]), op=ALU.mult,
                         reads=["big0", "i16f"], writes=["big0"])
                    T.op("dve", "tensor_reduce", out=sel[:, w_, :], in_=sc[:].rearrange("p (hk i) -> p hk i", i=16),
                         axis=AX.X, op=ALU.add, reads=["big0"], writes=["sel"])
                selg = sel[:, 2, :].rearrange("p (h k) -> p h k", h=8)
                T.op("dve", "tensor_tensor", out=selg, in0=s16[:], in1=s16[:, :, 0:1].to_broadcast([128, 8, 16]), op=ALU.subtract,
                     reads=["s16"], writes=["sel"])
                T.op("act", "activation", out=selg, in_=selg, func=AF.Exp, reads=["sel"], writes=["sel"])
                T.op("dve", "tensor_reduce", out=zz[:, :, 0], in_=selg, axis=AX.X, op=ALU.add, reads=["sel"], writes=["zz"])
                T.op("dve", "reciprocal", out=zz[:, :, 1], in_=zz[:, :, 0], reads=["zz"], writes=["zz"])
                T.op("dve", "tensor_tensor", out=selg, in0=selg, in1=zz[:, :, 1:2].to_broadcast([128, 8, 16]), op=ALU.mult,
                     reads=["sel", "zz"], writes=["sel"])
                b = self.nb()
                for w_ in range(3):
                    T.op("pe", "transpose", self.ps[b][:, w_ * 128:(w_ + 1) * 128], sel[:, w_, :], self.ident_f[:],
                         reads=["sel", "ident_f"], writes=[f"ps{b}"])
                T.op("act", "copy", out=selT[p][ti][:].rearrange("p w n -> p (w n)"), in_=self.ps[b][:, 0:384], reads=[f"ps{b}"], writes=[f"selT{p}{ti}"])
                yield

        def gbuild(sbi):
            p = sbi % 2
            tiles = [t for t in (2 * sbi, 2 * sbi + 1) if t < NTILE]
            for ti, t in enumerate(tiles):
                OAf = cand[:].rearrange("p (t a) -> p t a", t=16)
                for t0 in range(0, 128, 16):
                    iob = iota[:].unsqueeze(1).to_broadcast([128, 16, 128])
                    T.op("dve", "tensor_tensor", out=OAf, in0=iob,
                         in1=selT[p][ti][:, 0, t0:t0 + 16].unsqueeze(2).to_broadcast([128, 16, 128]), op=ALU.is_equal,
                         reads=["iota", f"selT{p}{ti}"], writes=["big2"])
                    T.op("dve", "tensor_tensor", out=OA[:], in0=OAf,
                         in1=selT[p][ti][:, 2, t0:t0 + 16].unsqueeze(2).to_broadcast([128, 16, 128]), op=ALU.mult,
                         reads=["big2", f"selT{p}{ti}"], writes=["OA"])
                    T.op("dve", "tensor_tensor", out=OB[:], in0=iob,
                         in1=selT[p][ti][:, 1, t0:t0 + 16].unsqueeze(2).to_broadcast([128, 16, 128]), op=ALU.is_equal,
                         reads=["iota", f"selT{p}{ti}"], writes=["OB"])
                    for g in range(4):
                        b = self.nb()
                        for j in range(4):
                            tt = g * 4 + j
                            T.op("pe", "matmul", self.ps[b][:, j * 128:(j + 1) * 128], OB[:, tt, :], OA[:, tt, :],
                                 start=True, stop=True, reads=["OA", "OB"], writes=[f"ps{b}"])
                        tk0 = ti * 128 + t0 + g * 4
                        T.op("act", "copy", out=GT[:, :, tk0:tk0 + 4], in_=self.ps[b][:].rearrange("p (t a) -> p a t", t=4),
                             reads=[f"ps{b}"], writes=["GT"])

        def stageb(sbi, gen):
            p = sbi % 2
            tiles = [t for t in (2 * sbi, 2 * sbi + 1) if t < NTILE]
            ntok = 128 * len(tiles)
            acc = self.reserve_banks(4)
            hbanks = self.reserve_banks(2)
            nt_ = len(tiles)

            def load(a):
                i = a % 3
                T.dma("sp", utb[i][:], S["UT"][a], reads=["DR_UT"], writes=[f"ut{i}"])
                T.dma("sp", vtb[i][:], S["VB"][a], reads=["DR_VB"], writes=[f"vt{i}"])

            def hmm(a):
                b = hbanks[a % 2]
                i = a % 3
                for dc in range(8):
                    T.op("pe", "matmul", self.ps[b][:, 0:ntok], utb[i][:, dc, :], xTs[p][:, dc, 0:ntok],
                         start=(dc == 0), stop=(dc == 7), reads=[f"ut{i}", f"xTs{p}"], writes=[f"ps{b}"])
                return b

            load(0)
            load(1)
            hbs = {0: hmm(0)}
            for a in range(128):
                if a + 2 < 128:
                    load(a + 2)
                if a + 1 < 128:
                    hbs[a + 1] = hmm(a + 1)
                hb = hbs.pop(a)
                j = a % 2
                T.op("act", "activation", out=hg[j][:, 0:ntok], in_=self.ps[hb][:, 0:ntok], func=AF.Gelu,
                     reads=[f"ps{hb}"], writes=[f"hg{j}"])
                T.op("dve", "tensor_tensor", out=GH[j][:, 0:ntok], in0=hg[j][:, 0:ntok], in1=GT[:, a, 0:ntok], op=ALU.mult,
                     reads=[f"hg{j}", "GT"], writes=[f"GH{j}"])
                i = a % 3
                for ti in range(nt_):
                    for half in range(2):
                        b = acc[ti * 2 + half]
                        T.op("pe", "matmul", self.ps[b][:], GH[j][:, ti * 128:(ti + 1) * 128], vtb[i][:, half * 512:(half + 1) * 512],
                             start=(a == 0), stop=(a == 127), reads=[f"GH{j}", f"vt{i}"], writes=[f"ps{b}"])
                if gen is not None and a % 2 == 0:
                    if next(gen, "done") == "done":
                        gen = None
            for ti, t in enumerate(tiles):
                r0 = t * 128
                for half in range(2):
                    b = acc[ti * 2 + half]
                    T.op("dve", "scalar_tensor_tensor", out=tin[:, half * 512:(half + 1) * 512],
                         in0=xt[p][ti][:, half * 512:(half + 1) * 512], scalar=ALPHA, in1=self.ps[b][:],
                         op0=ALU.mult, op1=ALU.add, reads=[f"xt{p}{ti}", f"ps{b}"], writes=["tin"])
                self.layer_norm_store(tin, "tin", gt, bt, xo[0], "xo0", tmp)
                T.dma("pool", xout_ap[r0:r0 + 128, :], xo[0][:], reads=["xo0"], writes=[xout_key])
            if gen is not None:
                for _ in gen:
                    pass
            self.release_banks()

        for _ in routing(0):
            pass
        for sbi in range(nsb):
            gbuild(sbi)
            gen = routing(sbi + 1) if sbi + 1 < nsb else None
            stageb(sbi, gen)

    def phase_qkv(self, es):
        nc, T, I, O, S = self.nc, self.T, self.I, self.O, self.S
        NT, SEQ = self.NT, self.SEQ
        sb = lambda name, shape, dt: es.enter_context(nc.sbuf_tensor("Q_" + name, list(shape), dt))
        wqkv = sb("wqkv", [128, 8, 3072], BF16)
        stage = {"t": [sb("wst0", [128, 8, 256], F32), sb("wst1", [128, 8, 256], F32)], "i": 0, "cols": 256}
        self.load_weight_bf16(es, wqkv, "wqkv", I["sb_w_qkv"], 3072, stage, 8)
        xt = [sb("xt0", [128, D], F32), sb("xt1", [128, D], F32)]
        xT = [sb("xT0", [128, 8, 128], BF16), sb("xT1", [128, 8, 128], BF16)]
        ko = [sb("ko0", [128, D], F32), sb("ko1", [128, D], F32)]
        vo = [sb("vo0", [128, D], F32), sb("vo1", [128, D], F32)]
        vbf = [sb("vbf0", [128, D], BF16), sb("vbf1", [128, D], BF16)]
        qTt = [sb("qTt0", [128, 8, 128], BF16), sb("qTt1", [128, 8, 128], BF16)]
        kTt = [sb("kTt0", [128, 8, 128], BF16), sb("kTt1", [128, 8, 128], BF16)]
        for t in range(self.NTILE):
            sl = t % 2
            r0 = t * 128
            xk, xTk = f"xt{sl}", f"xT{sl}"
            T.dma("sp", xt[sl][:], S["X2"][r0:r0 + 128, :], reads=["DR_X2"], writes=[xk])
            self.transpose_to_bf16(xt[sl], xk, xT[sl], xTk, 8)
            for which, dst, col_base in (("k", ko[sl], 1024), ("v", vo[sl], 2048)):
                for j in range(2):
                    b = self.nb()
                    for kc in range(8):
                        T.op("pe", "matmul", self.ps[b][:], xT[sl][:, kc, :], wqkv[:, kc, col_base + j * 512:col_base + (j + 1) * 512],
                             start=(kc == 0), stop=(kc == 7), reads=["wqkv", xTk], writes=[f"ps{b}"])
                    T.op("act", "copy", out=dst[:, j * 512:(j + 1) * 512], in_=self.ps[b][:], reads=[f"ps{b}"], writes=[f"{which}o{sl}"])
            T.op("pool", "tensor_copy", out=vbf[sl][:], in_=vo[sl][:], reads=[f"vo{sl}"], writes=[f"vbf{sl}"])
            T.dma("pool", S["VS"][r0:r0 + 128, :], vbf[sl][:], reads=[f"vbf{sl}"], writes=["DR_VS"])
            for which, src in (("k", ko[sl]), ("v", vo[sl])):
                if t < NT:
                    T.dma("pool", O[which + "p"][:, r0:r0 + 128, :].rearrange("h s d -> s h d"),
                          src[:].rearrange("p (h d) -> p h d", h=SBH), reads=[f"{which}o{sl}"])
                else:
                    for s_ in range(2):
                        T.dma("pool", O[which + "s"][s_].rearrange("h s d -> s h d"),
                              src[s_ * 64:(s_ + 1) * 64, :].rearrange("p (h d) -> p h d", h=SBH), reads=[f"{which}o{sl}"])
            for which, dst, col_base, scl, scr in (("q", qTt[sl], 0, SBD ** -0.5, "QT"), ("k", kTt[sl], 1024, 1.0, "KT")):
                for g in range(2):
                    b = self.nb()
                    for j in range(4):
                        c = g * 4 + j
                        for kc in range(8):
                            T.op("pe", "matmul", self.ps[b][:, j * 128:(j + 1) * 128], wqkv[:, kc, col_base + c * 128:col_base + (c + 1) * 128],
                                 xT[sl][:, kc, :], start=(kc == 0), stop=(kc == 7), reads=["wqkv", xTk], writes=[f"ps{b}"])
                    T.op("act", "mul", out=dst[:, g * 4:(g + 1) * 4, :], in_=self.ps[b][:].rearrange("p (c n) -> p c n", c=4),
                         mul=scl, reads=[f"ps{b}"], writes=[f"{which}Tt{sl}"])
                T.dma("pool", S[scr][:, :, r0:r0 + 128].rearrange("c p t -> p c t"), dst[:], reads=[f"{which}Tt{sl}"], writes=["DR_" + scr])

    def sb_stage1(self, blk, st):
        T = self.T
        nk = blk["nk"]
        a = self.nb()
        blk["a"] = a
        i = st["i"]
        for (lh, rh, c0, ncol, keys) in blk["zmms"]:
            T.op("pe", "matmul", self.ps[a][0:nk, c0:c0 + ncol], lh, rh, start=True, stop=(blk["negm"] is None),
                 reads=keys, writes=[f"ps{a}"])
            if blk["negm"] is not None:
                T.op("pe", "matmul", self.ps[a][0:nk, c0:c0 + ncol], self.ident_b[0:nk, 0:nk], blk["negm"][:, c0:c0 + ncol],
                     start=False, stop=True, reads=["ident_b", "negm"], writes=[f"ps{a}"])
        e, spt = st["e"][i % 2], st["sp"][i % 3]
        ek, spk = f"sbe{i % 2}", f"sbsp{i % 3}"
        T.op("act", "activation", out=e[0:nk, :], in_=self.ps[a][0:nk, :], func=AF.Exp, reads=[f"ps{a}"], writes=[ek])
        T.op("act", "activation", out=spt[0:nk, :], in_=e[0:nk, :], func=AF.Ln, bias=1.0, scale=1.0, reads=[ek], writes=[spk])
        blk["spt"], blk["spk"] = spt, spk
        nxt = (i + 1) % 3
        if i == 0:
            if nk < 128:
                T.op("pool", "memset", st["SPf"][:], 0.0, writes=["sbSPf"])
            T.op("dve", "tensor_copy", out=st["SPf"][0:nk, :], in_=spt[0:nk, :], reads=[spk], writes=["sbSPf"])
        else:
            T.op("dve", "tensor_tensor", out=st["SPf"][0:nk, :], in0=st["SPf"][0:nk, :], in1=spt[0:nk, :], op=ALU.add,
                 reads=["sbSPf", spk], writes=["sbSPf"])
        T.op("dve", "tensor_copy", out=st["SPb"][nxt][:], in_=st["SPf"][:], reads=["sbSPf"], writes=[f"sbSPb{nxt}"])
        blk["i"] = i
        st["i"] = i + 1

    def sb_stage2(self, blk, st, nblocks):
        T = self.T
        nk, i = blk["nk"], blk["i"]
        b = self.nb()
        for (lh, rh, c0, ncol, keys) in blk["zmms"]:
            T.op("pe", "matmul", self.ps[b][0:nk, c0:c0 + ncol], lh, rh, start=True, stop=False, reads=keys, writes=[f"ps{b}"])
            if blk["negm"] is not None:
                T.op("pe", "matmul", self.ps[b][0:nk, c0:c0 + ncol], self.ident_b[0:nk, 0:nk], blk["negm"][:, c0:c0 + ncol],
                     start=False, stop=False, reads=["ident_b", "negm"], writes=[f"ps{b}"])
            T.op("pe", "matmul", self.ps[b][0:nk, c0:c0 + ncol], self.tri[0:nk, 0:nk], blk["spt"][0:nk, c0:c0 + ncol],
                 start=False, stop=(i == 0), reads=["sbtri", blk["spk"]], writes=[f"ps{b}"])
            if i > 0:
                T.op("pe", "matmul", self.ps[b][0:nk, c0:c0 + ncol], self.negones[:, 0:nk], st["SPb"][i % 3][:, c0:c0 + ncol],
                     start=False, stop=True, reads=["sbones", f"sbSPb{i % 3}"], writes=[f"ps{b}"])
        W = st["W"][i % 2]
        wk = f"sbW{i % 2}"
        T.op("act", "activation", out=W[0:nk, :], in_=self.ps[b][0:nk, :], func=AF.Exp, reads=[f"ps{b}"], writes=[wk])
        if "DBG_oT" in self.debug and not getattr(self, "_dbgW", False):
            self._dbgW = True
            dd = st["e"][(i + 1) % 2]
            dk = f"sbe{(i + 1) % 2}"
            T.dma("pool", self.S["DBG_W"][0], st["e"][i % 2][:], reads=[f"sbe{i % 2}"], writes=["DR_dbg"])
            T.op("dve", "tensor_copy", out=dd[:], in_=blk["spt"][:], reads=[blk["spk"]], writes=[dk])
            T.dma("pool", self.S["DBG_W"][1], dd[:], reads=[dk], writes=["DR_dbg"])
            T.op("dve", "tensor_copy", out=dd[:], in_=W[:], reads=[wk], writes=[dk])
            T.dma("pool", self.S["DBG_W"][2], dd[:], reads=[dk], writes=["DR_dbg"])
            T.op("dve", "tensor_copy", out=dd[:], in_=self.ps[b][:], reads=[f"ps{b}"], writes=[dk])
            T.dma("pool", self.S["DBG_W"][3], dd[:], reads=[dk], writes=["DR_dbg"])
        ob = st["obank"]
        if i == 0:
            T.op("pe", "matmul", self.ps[ob][:], self.ident_b[:], self.zeros_b[:], start=True, stop=False,
                 reads=["ident_b", "zeros_b"], writes=[f"ps{ob}"])
        for (lh, c0, ncol, oc0, keys) in blk["avmms"]:
            T.op("pe", "matmul", self.ps[ob][:, oc0:oc0 + ncol], lh, W[0:nk, c0:c0 + ncol], start=False, stop=(i == nblocks - 1),
                 reads=keys + [wk], writes=[f"ps{ob}"])

    def sb_run(self, blocks, st):
        st["i"] = 0
        n = len(blocks)

        def s1(blk):
            if blk.get("pre") is not None:
                blk["pre"]()
            self.sb_stage1(blk, st)

        s1(blocks[0])
        for i in range(n):
            if i + 1 < n:
                s1(blocks[i + 1])
            self.sb_stage2(blocks[i], st, n)

    def phase_att(self, es):
        nc, T, I, O, S = self.nc, self.T, self.I, self.O, self.S
        NT, SEQ, PAST = self.NT, self.SEQ, self.PAST
        sb = lambda name, shape, dt: es.enter_context(nc.sbuf_tensor("AT_" + name, list(shape), dt))
        wo = sb("wo", [128, 8, 1024], BF16)
        stage = {"t": [sb("wst0", [128, 8, 256], F32), sb("wst1", [128, 8, 256], F32)], "i": 0, "cols": 256}
        self.load_weight_bf16(es, wo, "wo", I["sb_w_o"], 1024, stage, 8)
        gt = sb("lng", [128, D], F32)
        bt = sb("lnb", [128, D], F32)
        T.dma("sp", gt[:], I["ln_g"][2:3, :].partition_broadcast(128)[:, 0, :], writes=["lng"])
        T.dma("sp", bt[:], I["ln_b"][2:3, :].partition_broadcast(128)[:, 0, :], writes=["lnb"])
        cf = stage["t"][0][:].rearrange("p a b -> p (a b)")
        self.tri = sb("tri", [128, 128], BF16)
        self.negones = sb("negones", [128, 128], BF16)
        negm_p = sb("negm_p", [128, 512], BF16)
        negm_s = sb("negm_s", [64, 512], BF16)
        T.dma("sp", cf[:, 0:128], I["tri"], writes=["wstage0"])
        T.dma("sp", cf[:, 128:640], I["negm_p"], writes=["wstage0"])
        T.dma("sp", cf[0:64, 640:1152], I["negm_s"], writes=["wstage0"])
        T.op("dve", "tensor_copy", out=self.tri[:], in_=cf[:, 0:128], reads=["wstage0"], writes=["sbtri"])
        T.op("dve", "tensor_copy", out=negm_p[:], in_=cf[:, 128:640], reads=["wstage0"], writes=["negm"])
        T.op("dve", "tensor_copy", out=negm_s[:], in_=cf[0:64, 640:1152], reads=["wstage0"], writes=["negm"])
        T.op("pool", "memset", self.negones[:], -1.0, writes=["sbones"])
        self.zeros_b = sb("zeros_b", [128, 512], BF16)
        T.op("pool", "memset", self.zeros_b[:], 0.0, writes=["zeros_b"])
        st = {"e": [sb("e0", [128, 512], F32), sb("e1", [128, 512], F32)],
              "sp": [sb(f"sp{i}", [128, 512], BF16) for i in range(3)],
              "W": [sb("W0", [128, 512], BF16), sb("W1", [128, 512], BF16)],
              "SPf": sb("SPf", [128, 512], F32),
              "SPb": [sb(f"SPb{i}", [128, 512], BF16) for i in range(3)]}
        KB = 8
        ktb = [sb(f"ktb{i}", [128, 2, KB * 128], BF16) for i in range(3)]
        vtb = [sb(f"vtb{i}", [128, KB, 256], BF16) for i in range(3)]
        qt = [sb("qt0", [128, 8, 128], BF16), sb("qt1", [128, 8, 128], BF16)]
        QZ = sb("QZ", [128, 8, 256], BF16)
        QZs = sb("QZs", [128, 8, 2, 128], BF16)
        T.op("pool", "memset", QZ[:], 0.0, writes=["QZ"])
        T.op("pool", "memset", QZs[:], 0.0, writes=["QZs"])
        oT = sb("oT", [128, 8, 128], BF16)
        xt = [sb("xt0", [128, D], F32), sb("xt1", [128, D], F32)]
        tin = sb("tin", [128, D], F32)
        xo = [sb("xo0", [128, D], F32), sb("xo1", [128, D], F32)]
        tmp = {"bst": sb("ln_bst", [128, 2, 6], F32), "mv": sb("ln_mv", [128, 2], F32), "rs": sb("ln_rs", [128, 2], F32)}
        ckf = [sb("ckf0", [128, 8, 64], F32), sb("ckf1", [128, 8, 64], F32)]
        cvf = [sb("cvf0", [128, 8, 64], F32), sb("cvf1", [128, 8, 64], F32)]
        ckT = [sb("ckT0", [128, 4, 128], BF16), sb("ckT1", [128, 4, 128], BF16)]
        cvb = [sb("cvb0", [128, 512], BF16), sb("cvb1", [128, 512], BF16)]
        ldi = 0
        for t in range(self.NTILE):
            sample = (t == NT)
            sl = t % 2
            r0 = t * 128
            T.dma("sp", xt[sl][:], S["X2"][r0:r0 + 128, :], reads=["DR_X2"], writes=[f"xt{sl}"])
            T.dma("sp", qt[sl][:], S["QT"][:, :, r0:r0 + 128].rearrange("c p t -> p c t"), reads=["DR_QT"], writes=[f"qt{sl}"])
            if not sample:
                T.op("pool", "tensor_copy", out=QZ[0:64, :, 0:128], in_=qt[sl][0:64, :, :], reads=[f"qt{sl}"], writes=["QZ"])
                T.op("pool", "tensor_copy", out=QZ[64:128, :, 128:256], in_=qt[sl][64:128, :, :], reads=[f"qt{sl}"], writes=["QZ"])
                for hg in range(4):
                    st["obank"] = self.reserve_banks(1)[0]
                    blocks = []
                    nb_total = t + 1
                    batches = []
                    jhi = t
                    while jhi >= 0:
                        jlo = max(0, jhi - KB + 1)
                        batches.append((jlo, jhi))
                        jhi = jlo - 1

                    def mk_load(jlo, jhi, bi, hg=hg):
                        def f():
                            nblk = jhi - jlo + 1
                            T.dma("sp", ktb[bi][:, :, 0:nblk * 128],
                                  S["KT"][2 * hg:2 * hg + 2, :, jlo * 128:(jhi + 1) * 128].rearrange("c p t -> p c t"),
                                  reads=["DR_KT"], writes=[f"ktb{bi}"])
                            T.dma("sp", vtb[bi][:, 0:nblk, :],
                                  S["VS"][jlo * 128:(jhi + 1) * 128, hg * 256:(hg + 1) * 256].rearrange("(n p) c -> p n c", p=128),
                                  reads=["DR_VS"], writes=[f"vtb{bi}"])
                        return f

                    bis = []
                    for (jlo, jhi) in batches:
                        bis.append(ldi % 3)
                        ldi += 1
                    mk_load(batches[0][0], batches[0][1], bis[0])()
                    for n_, (jlo, jhi) in enumerate(batches):
                        bi = bis[n_]
                        for j in range(jhi, jlo - 1, -1):
                            jj = j - jlo
                            zmms = [(ktb[bi][:, cl, jj * 128:(jj + 1) * 128], QZ[:, 2 * hg + cl, :], cl * 256, 256, [f"ktb{bi}", "QZ"])
                                    for cl in range(2)]
                            avmms = [(vtb[bi][:, jj, cl * 128:(cl + 1) * 128], cl * 256, 256, cl * 256, [f"vtb{bi}"]) for cl in range(2)]
                            pre = None
                            if j == jhi and n_ + 1 < len(batches):
                                pre = mk_load(batches[n_ + 1][0], batches[n_ + 1][1], bis[n_ + 1])
                            blocks.append({"nk": 128, "zmms": zmms, "avmms": avmms, "negm": negm_p[:] if j == t else None, "pre": pre})
                    self.sb_run_batched(blocks, st)
                    ob = st["obank"]
                    for cl in range(2):
                        c = 2 * hg + cl
                        T.op("act", "copy", out=oT[0:64, c, :], in_=self.ps[ob][0:64, cl * 256:cl * 256 + 128],
                             reads=[f"ps{ob}"], writes=["oT"])
                        T.op("act", "copy", out=oT[64:128, c, :], in_=self.ps[ob][64:128, cl * 256 + 128:cl * 256 + 256],
                             reads=[f"ps{ob}"], writes=["oT"])
                    self.release_banks()
            else:
                for s_ in range(2):
                    T.op("pool", "tensor_copy", out=QZs[0:64, :, s_, 0:64], in_=qt[sl][0:64, :, s_ * 64:(s_ + 1) * 64],
                         reads=[f"qt{sl}"], writes=["QZs"])
                    T.op("pool", "tensor_copy", out=QZs[64:128, :, s_, 64:128], in_=qt[sl][64:128, :, s_ * 64:(s_ + 1) * 64],
                         reads=[f"qt{sl}"], writes=["QZs"])
                npast = PAST // 128
                for s_ in range(2):
                    for hg2 in range(2):
                        st["obank"] = self.reserve_banks(1)[0]
                        st["i"] = 0
                        nblocks = npast + 1
                        pending = None
                        for bidx in range(nblocks):
                            bi = bidx % 2
                            if bidx == 0:
                                k0 = SEQ + s_ * 64
                                T.dma("sp", ktb[bi][:, 0, 0:256].rearrange("p (c k) -> p c k", c=4),
                                      S["KT"][4 * hg2:4 * hg2 + 4, :, k0:k0 + 64].rearrange("c p t -> p c t"),
                                      reads=["DR_KT"], writes=[f"ktb{bi}"])
                                T.dma("sp", vtb[bi][0:64, 0:2, :].rearrange("p a b -> p (a b)"),
                                      S["VS"][k0:k0 + 64, hg2 * 512:(hg2 + 1) * 512], reads=["DR_VS"], writes=[f"vtb{bi}"])
                                nk = 64
                                kts = [ktb[bi][:, 0, pl * 64:(pl + 1) * 64] for pl in range(4)]
                                vsrc = vtb[bi][0:64, 0:2, :].rearrange("p a b -> p (a b)")
                                kkeys, vkeys = [f"ktb{bi}"], [f"vtb{bi}"]
                                negm = negm_s[:]
                            else:
                                j = npast - bidx
                                T.dma("sp", ckf[bi][:], I["cache_k"][s_, 8 * hg2:8 * hg2 + 8, j * 128:(j + 1) * 128, :].rearrange("h k d -> k h d"),
                                      writes=[f"ckf{bi}"])
                                T.dma("sp", cvf[bi][:], I["cache_v"][s_, 8 * hg2:8 * hg2 + 8, j * 128:(j + 1) * 128, :].rearrange("h k d -> k h d"),
                                      writes=[f"cvf{bi}"])
                                self.transpose_to_bf16(ckf[bi][:].rearrange("p h d -> p (h d)"), f"ckf{bi}", ckT[bi], f"ckT{bi}", 4)
                                T.op("dve", "tensor_copy", out=cvb[bi][:], in_=cvf[bi][:].rearrange("p h d -> p (h d)"),
                                     reads=[f"cvf{bi}"], writes=[f"cvb{bi}"])
                                nk = 128
                                kts = [ckT[bi][:, pl, :] for pl in range(4)]
                                vsrc = cvb[bi][:]
                                kkeys, vkeys = [f"ckT{bi}"], [f"cvb{bi}"]
                                negm = None
                            zmms = [(kts[pl], QZs[:, 4 * hg2 + pl, s_, :], pl * 128, 128, kkeys + ["QZs"]) for pl in range(4)]
                            avmms = [(vsrc[:, pl * 128:(pl + 1) * 128], pl * 128, 128, pl * 128, vkeys) for pl in range(4)]
                            blk = {"nk": nk, "zmms": zmms, "avmms": avmms, "negm": negm}
                            self.sb_stage1(blk, st)
                            if pending is not None:
                                self.sb_stage2(pending, st, nblocks)
                            pending = blk
                        self.sb_stage2(pending, st, nblocks)
                        ob = st["obank"]
                        for pl in range(4):
                            c = 4 * hg2 + pl
                            T.op("act", "copy", out=oT[0:64, c, s_ * 64:(s_ + 1) * 64], in_=self.ps[ob][0:64, pl * 128:pl * 128 + 64],
                                 reads=[f"ps{ob}"], writes=["oT"])
                            T.op("act", "copy", out=oT[64:128, c, s_ * 64:(s_ + 1) * 64], in_=self.ps[ob][64:128, pl * 128 + 64:pl * 128 + 128],
                                 reads=[f"ps{ob}"], writes=["oT"])
                        self.release_banks()
            if "DBG_oT" in self.debug:
                T.dma("pool", S["DBG_oT"][t], oT[:], reads=["oT"], writes=["DR_dbg"])
            for j in range(2):
                b = self.nb()
                for kc in range(8):
                    T.op("pe", "matmul", self.ps[b][:], oT[:, kc, :], wo[:, kc, j * 512:(j + 1) * 512],
                         start=(kc == 0), stop=(kc == 7), reads=["wo", "oT"], writes=[f"ps{b}"])
                T.op("dve", "scalar_tensor_tensor", out=tin[:, j * 512:(j + 1) * 512], in0=xt[sl][:, j * 512:(j + 1) * 512],
                     scalar=ALPHA, in1=self.ps[b][:], op0=ALU.mult, op1=ALU.add, reads=[f"xt{sl}", f"ps{b}"], writes=["tin"])
            self.layer_norm_store(tin, "tin", gt, bt, xo[sl], f"xo{sl}", tmp)
            T.dma("pool", S["X3"][r0:r0 + 128, :], xo[sl][:], reads=[f"xo{sl}"], writes=["DR_X3"])

    def sb_run_batched(self, blocks, st):
        self.sb_run(blocks, st)


def _shard_inputs(inp, SEQ, PAST, consts):
    maps = []
    for c in range(NCORES):
        xin = np.concatenate([inp["x_prompt"][c], inp["x_sample"][2 * c:2 * c + 2].reshape(128, D)], axis=0)
        m = {
            "xin": np.ascontiguousarray(xin, dtype=np.float32),
            "state_in": np.ascontiguousarray(inp["state_ret"][0, 2 * c:2 * c + 2]),
            "ret_w_in": np.ascontiguousarray(inp["ret_w_in"][0]),
            "ret_w_o": np.ascontiguousarray(inp["ret_w_o"][0]),
            "ln_g": np.ascontiguousarray(inp["ln_g"].reshape(4, D)),
            "ln_b": np.ascontiguousarray(inp["ln_b"].reshape(4, D)),
        }
        for k in ("peer_w_q", "peer_keys_a", "peer_keys_b", "peer_u", "peer_v"):
            m[k] = np.ascontiguousarray(inp[k])
        m["sb_w_qkv"] = np.ascontiguousarray(inp["sb_w_qkv"][0])
        m["sb_w_o"] = np.ascontiguousarray(inp["sb_w_o"][0])
        m["cache_k"] = np.ascontiguousarray(inp["cache_k"][0, 2 * c:2 * c + 2])
        m["cache_v"] = np.ascontiguousarray(inp["cache_v"][0, 2 * c:2 * c + 2])
        for k, v in consts.items():
            if isinstance(v, np.ndarray):
                m[k] = v
        maps.append(m)
    return maps


def run(inp, SEQ, PAST, debug=(), upto=9):
    import time
    t0 = time.time()
    bld = Builder(SEQ, PAST, debug, upto)
    nc = bld.build()
    print("build time", time.time() - t0, "nins", bld.T.nins, "nwait", bld.T.nwait, flush=True)
    maps = _shard_inputs(inp, SEQ, PAST, bld.C)
    res = run_bass_kernel_spmd(nc, maps, core_ids=list(range(NCORES)))
    return bld, res.results


def kernel(**inp):
    inp = {k: np.asarray(v) for k, v in inp.items()}
    SEQ = inp["x_prompt"].shape[1]
    PAST = inp["cache_k"].shape[3]
    bld, r = run(inp, SEQ, PAST)
    B = inp["x_prompt"].shape[0]
    y_p = np.stack([r[c]["y"][:SEQ] for c in range(NCORES)])
    y_s = np.concatenate([r[c]["y"][SEQ:].reshape(2, DEC_SEQ, D) for c in range(NCORES)])
    ret_p = np.stack([r[c]["ret_p"] for c in range(NCORES)])[None]
    ret_s = np.concatenate([r[c]["ret_s"] for c in range(NCORES)])[None]
    kp = np.stack([r[c]["kp"] for c in range(NCORES)])[None]
    vp = np.stack([r[c]["vp"] for c in range(NCORES)])[None]
    ks = np.concatenate([r[c]["ks"] for c in range(NCORES)])[None]
    vs = np.concatenate([r[c]["vs"] for c in range(NCORES)])[None]
    return tuple(np.ascontiguousarray(a, dtype=np.float32) for a in (y_p, y_s, ret_p, ret_s, kp, vp, ks, vs))
```
